# Optimizing a Trainium2 kernel written in Bass

```python
import math
import jax
import jax.numpy as jnp
from jax import lax
import numpy as np

D_MODEL = 1024
BATCH = 4
SEQ = 4096
DEPTH = 4

N_EVEN = (DEPTH + 1) // 2
N_ODD = DEPTH // 2
RMS_EPS = 1e-6

A_WIDTH = D_MODEL
A_HEAD_DIM = 64
A_HEADS = A_WIDTH // A_HEAD_DIM
A_PATTERNS = ((128, 1), (512, 4), (2048, 16))
A_BLOCK = 128

B_WIDTH = D_MODEL
B_EXPAND = 128
B_HEADS = B_WIDTH // B_EXPAND
B_HEAD_DIM = B_WIDTH // B_HEADS
B_CHUNK = 64

EVEN_SIZES = (A_WIDTH, A_WIDTH, A_WIDTH, A_WIDTH, B_WIDTH, B_WIDTH, B_WIDTH, B_WIDTH)
EVEN_IN = sum(EVEN_SIZES)
EVEN_MIX = A_WIDTH + B_WIDTH

C_HEAD_K = 128
C_HEAD_V = 128
C_K_HEADS = D_MODEL // C_HEAD_K
C_V_HEADS = 2 * C_K_HEADS
C_KEY_WIDTH = C_K_HEADS * C_HEAD_K
C_VAL_WIDTH = C_V_HEADS * C_HEAD_V
C_CONV = 4
C_CHUNK = 64
C_CONV_CH = 2 * C_KEY_WIDTH + C_VAL_WIDTH
ODD_SIZES = (C_CONV_CH, C_VAL_WIDTH, C_V_HEADS, C_V_HEADS)
ODD_IN = sum(ODD_SIZES)

kernel_name = 'hybrid_dilated_hgrn2_gdn_trunk'


def _split(t, sizes):
    return jnp.split(t, np.cumsum(sizes)[:-1].tolist(), axis=-1)


def rms_norm(x, w):
    xf = x.astype(jnp.float32)
    y = xf * lax.rsqrt(jnp.mean(xf * xf, axis=-1, keepdims=True) + RMS_EPS)
    return (y * w.astype(jnp.float32)).astype(x.dtype)


def l2_normalize(x):
    return x * lax.rsqrt(jnp.sum(x * x, axis=-1, keepdims=True) + RMS_EPS)


def dilated_branch(q, k, v, window, dilation):
    bsz, s, h, dh = q.shape
    L = s // dilation
    nb = -(-L // A_BLOCK)
    Lp = nb * A_BLOCK
    steps = window // dilation

    def to_residue(t):
        t = t.reshape(bsz, L, dilation, h, dh).transpose(0, 2, 1, 3, 4)
        return jnp.pad(t, ((0, 0), (0, 0), (0, Lp - L), (0, 0), (0, 0)))

    def key_windows(t):
        t = jnp.pad(to_residue(t), ((0, 0), (0, 0), (A_BLOCK, 0), (0, 0), (0, 0)))
        t = t.reshape(bsz, dilation, nb + 1, A_BLOCK, h, dh)
        return jnp.concatenate([t[:, :, :-1], t[:, :, 1:]], axis=3)

    qr = to_residue(q).reshape(bsz, dilation, nb, A_BLOCK, h, dh)
    kw = key_windows(k)
    vw = key_windows(v)
    qpos = jnp.arange(nb)[:, None] * A_BLOCK + jnp.arange(A_BLOCK)[None, :]
    kpos = jnp.arange(nb)[:, None] * A_BLOCK - A_BLOCK + jnp.arange(2 * A_BLOCK)[None, :]
    dist = qpos[:, :, None] - kpos[:, None, :]
    valid = (dist >= 0) & (dist <= steps) & (kpos[:, None, :] >= 0)

    sc = jnp.einsum('brnqhd,brnkhd->brnhqk', qr, kw).astype(jnp.float32) * (A_HEAD_DIM ** -0.5)
    sc = jnp.where(valid[None, None, :, None], sc, -jnp.inf)
    m = jnp.max(sc, axis=-1, keepdims=True)
    p = jnp.exp(sc - m)
    den = jnp.sum(p, axis=-1, keepdims=True)
    o = jnp.einsum('brnhqk,brnkhd->brnqhd', p / den, vw.astype(jnp.float32))
    lse = jnp.swapaxes((m + jnp.log(den))[..., 0], -1, -2)

    def from_residue(t):
        t = t.reshape((bsz, dilation, Lp) + t.shape[4:])[:, :, :L]
        t = jnp.swapaxes(t, 1, 2)
        return t.reshape((bsz, s) + t.shape[3:])

    return from_residue(o), from_residue(lse)


def dilated_mixture_attention(q, k, v):
    outs, lses = [], []
    for window, dilation in A_PATTERNS:
        o, lse = dilated_branch(q, k, v, window, dilation)
        outs.append(o)
        lses.append(lse)
    wts = jax.nn.softmax(jnp.stack(lses), axis=0)
    return jnp.einsum('gbsh,gbshd->bshd', wts, jnp.stack(outs))


def hgrn2_chunked(q, f_logit, i, lb, norm_w):
    bsz, s, h, e = q.shape
    n = s // B_CHUNK
    q = q.astype(jnp.float32)
    i = i.astype(jnp.float32)
    lbh = lb.reshape(h, e)
    log_f = jnp.logaddexp(jnp.log(lbh), jnp.log1p(-lbh) + jax.nn.log_sigmoid(f_logit.astype(jnp.float32)))
    k = -jnp.expm1(log_f)

    def chunks(t):
        return t.reshape(bsz, n, B_CHUNK, h, t.shape[-1]).transpose(1, 0, 3, 2, 4)

    qc, kc, vc = chunks(q), chunks(k), chunks(i)
    gc = jnp.cumsum(chunks(log_f), axis=-2)
    causal = jnp.tril(jnp.ones((B_CHUNK, B_CHUNK), dtype=bool))

    def step(state, inp):
        qb, kb, vb, gb = inp
        o_inter = jnp.einsum('bhce,bhev->bhcv', qb * jnp.exp(gb), state)
        diff = gb[:, :, :, None, :] - gb[:, :, None, :, :]
        decay = jnp.exp(jnp.where(causal[:, :, None], diff, -jnp.inf))
        attn = jnp.einsum('bhie,bhje,bhije->bhij', qb, kb, decay)
        o = o_inter + jnp.einsum('bhij,bhjv->bhiv', attn, vb)
        g_last = gb[:, :, -1]
        k_tail = kb * jnp.exp(g_last[:, :, None, :] - gb)
        new_state = state * jnp.exp(g_last)[..., None] + jnp.einsum('bhce,bhcv->bhev', k_tail, vb)
        return new_state, o

    state0 = jnp.zeros((bsz, h, e, i.shape[-1]), jnp.float32)
    _, o = lax.scan(step, state0, (qc, kc, vc, gc))
    o = o.transpose(1, 0, 3, 2, 4).reshape(bsz, s, h, i.shape[-1])
    return rms_norm(o, norm_w)


def even_mixer(hn, w_in, w_out, lb, hgrn_norm_w):
    bsz, s, _ = hn.shape
    aq, ak, av, ag, bq, bf, bi, bg = _split(hn @ w_in, EVEN_SIZES)
    shp_a = (bsz, s, A_HEADS, A_HEAD_DIM)
    a_out = dilated_mixture_attention(aq.reshape(shp_a), ak.reshape(shp_a), av.reshape(shp_a))
    b_out = hgrn2_chunked(bq.reshape(bsz, s, B_HEADS, B_EXPAND), bf.reshape(bsz, s, B_HEADS, B_EXPAND),
                          bi.reshape(bsz, s, B_HEADS, B_HEAD_DIM), lb, hgrn_norm_w)
    mixed = jnp.concatenate([
        a_out.reshape(bsz, s, A_WIDTH).astype(hn.dtype) * jax.nn.silu(ag),
        b_out.reshape(bsz, s, B_WIDTH).astype(hn.dtype) * jax.nn.silu(bg)], axis=-1)
    return mixed @ w_out


def causal_depthwise_conv(x, w):
    return lax.conv_general_dilated(x, w[:, None, :].astype(x.dtype), window_strides=(1,),
                                    padding=[(w.shape[0] - 1, 0)],
                                    dimension_numbers=('NWC', 'WIO', 'NWC'),
                                    feature_group_count=x.shape[-1])


def gated_delta_rule_chunked(q, k, v, g, beta):
    bsz, s, h, dk = q.shape
    dv = v.shape[-1]
    n = s // C_CHUNK

    def chunks(t):
        return t.reshape(bsz, n, C_CHUNK, h, -1).transpose(0, 3, 1, 2, 4)

    qc, kc, vc = chunks(q), chunks(k), chunks(v)
    bc = chunks(beta[..., None])
    gcum = jnp.cumsum(chunks(g[..., None])[..., 0], axis=-1)
    causal = jnp.tril(jnp.ones((C_CHUNK, C_CHUNK), dtype=bool))
    strict = jnp.tril(jnp.ones((C_CHUNK, C_CHUNK), dtype=bool), -1)
    decay = jnp.exp(jnp.where(causal, gcum[..., :, None] - gcum[..., None, :], -jnp.inf))
    kb = kc * bc
    lower = jnp.where(strict, jnp.einsum('bhnid,bhnjd->bhnij', kb, kc) * decay, 0.0)
    a_mat = lower + jnp.eye(C_CHUNK, dtype=jnp.float32)
    rhs = jnp.concatenate([vc * bc, kb * jnp.exp(gcum)[..., None]], axis=-1)
    sol = lax.linalg.triangular_solve(a_mat, rhs, left_side=True, lower=True, unit_diagonal=True)
    u, w = sol[..., :dv], sol[..., dv:]
    attn_qk = jnp.einsum('bhnid,bhnjd->bhnij', qc, kc) * decay
    q_dec = qc * jnp.exp(gcum)[..., None]
    k_tail = kc * jnp.exp(gcum[..., -1:] - gcum)[..., None]
    g_last = jnp.exp(gcum[..., -1])

    def step(state, inp):
        u_n, w_n, a_n, qd_n, kt_n, gl_n = inp
        v_new = u_n - jnp.einsum('bhcd,bhdv->bhcv', w_n, state)
        o = jnp.einsum('bhcd,bhdv->bhcv', qd_n, state) + jnp.einsum('bhij,bhjv->bhiv', a_n, v_new)
        new_state = state * gl_n[..., None, None] + jnp.einsum('bhcd,bhcv->bhdv', kt_n, v_new)
        return new_state, o

    xs = tuple(jnp.moveaxis(t, 2, 0) for t in (u, w, attn_qk, q_dec, k_tail, g_last))
    state0 = jnp.zeros((bsz, h, dk, dv), jnp.float32)
    _, o = lax.scan(step, state0, xs)
    return o.transpose(1, 0, 3, 2, 4).reshape(bsz, s, h, dv)


def odd_mixer(hn, w_in, conv_w, dt_bias, a_log, norm_w, w_out):
    bsz, s, _ = hn.shape
    qkv, z, b_logit, a_logit = _split(hn @ w_in, ODD_SIZES)
    qkv = jax.nn.silu(causal_depthwise_conv(qkv, conv_w)).astype(jnp.float32)
    q, k, v = _split(qkv, (C_KEY_WIDTH, C_KEY_WIDTH, C_VAL_WIDTH))
    rep = C_V_HEADS // C_K_HEADS
    q = jnp.repeat(l2_normalize(q.reshape(bsz, s, C_K_HEADS, C_HEAD_K)), rep, axis=2) * (C_HEAD_K ** -0.5)
    k = jnp.repeat(l2_normalize(k.reshape(bsz, s, C_K_HEADS, C_HEAD_K)), rep, axis=2)
    v = v.reshape(bsz, s, C_V_HEADS, C_HEAD_V)
    beta = jax.nn.sigmoid(b_logit.astype(jnp.float32))
    g = -jnp.exp(a_log.astype(jnp.float32)) * jax.nn.softplus(a_logit.astype(jnp.float32) + dt_bias.astype(jnp.float32))
    o = gated_delta_rule_chunked(q, k, v, g, beta)
    o = rms_norm(o, norm_w) * jax.nn.silu(z.astype(jnp.float32).reshape(bsz, s, C_V_HEADS, C_HEAD_V))
    return o.reshape(bsz, s, C_VAL_WIDTH).astype(hn.dtype) @ w_out


def setup_inputs(seed: int = 0) -> dict:
    key = jax.random.key(seed)
    ks = jax.random.split(key, 16)
    f32 = jnp.float32
    nrm = lambda k, shp, sc: jax.random.normal(k, shp, f32) * sc
    dt = jnp.exp(jax.random.uniform(ks[10], (N_ODD, C_V_HEADS), f32, math.log(1e-3), math.log(1e-1)))
    return {
        'x': nrm(ks[0], (BATCH, SEQ, D_MODEL), 1.0),
        'norm_w': 1.0 + nrm(ks[1], (DEPTH, D_MODEL), 0.02),
        'final_norm_w': 1.0 + nrm(ks[2], (D_MODEL,), 0.02),
        'even_w_in': nrm(ks[3], (N_EVEN, D_MODEL, EVEN_IN), D_MODEL ** -0.5),
        'even_w_out': nrm(ks[4], (N_EVEN, EVEN_MIX, D_MODEL), EVEN_MIX ** -0.5),
        'hgrn_lb_logits': nrm(ks[5], (N_EVEN, B_HEADS * B_EXPAND), 0.1),
        'hgrn_norm_w': 1.0 + nrm(ks[6], (N_EVEN, B_HEAD_DIM), 0.02),
        'odd_w_in': nrm(ks[7], (N_ODD, D_MODEL, ODD_IN), D_MODEL ** -0.5),
        'odd_conv_w': nrm(ks[8], (N_ODD, C_CONV, C_CONV_CH), C_CONV ** -0.5),
        'odd_dt_bias': jnp.log(jnp.expm1(dt)),
        'odd_a_log': jnp.log(jax.random.uniform(ks[9], (N_ODD, C_V_HEADS), f32, 1.0, 16.0)),
        'odd_norm_w': 1.0 + nrm(ks[11], (N_ODD, C_HEAD_V), 0.02),
        'odd_w_out': nrm(ks[12], (N_ODD, C_VAL_WIDTH, D_MODEL), C_VAL_WIDTH ** -0.5),
    }


def reference(x, norm_w, final_norm_w, even_w_in, even_w_out, hgrn_lb_logits, hgrn_norm_w,
              odd_w_in, odd_conv_w, odd_dt_bias, odd_a_log, odd_norm_w, odd_w_out):
    lb_all = jnp.cumsum(jax.nn.softmax(hgrn_lb_logits.astype(jnp.float32), axis=0), axis=0)
    lb_all = jnp.maximum(lb_all - lb_all[0:1], 0.0)
    h = x
    for layer in range(DEPTH):
        hn = rms_norm(h, norm_w[layer])
        j = layer // 2
        if layer % 2 == 0:
            h = h + even_mixer(hn, even_w_in[j], even_w_out[j], lb_all[j], hgrn_norm_w[j])
        else:
            h = h + odd_mixer(hn, odd_w_in[j], odd_conv_w[j], odd_dt_bias[j], odd_a_log[j],
                              odd_norm_w[j], odd_w_out[j])
    return rms_norm(h, final_norm_w)
```

```python
import contextlib
import numpy as np
import ml_dtypes
import concourse.bass as bass
import concourse.mybir as mybir
from concourse.bass_utils import run_bass_kernel_spmd

F32 = mybir.dt.float32
BF16 = mybir.dt.bfloat16
AF = mybir.ActivationFunctionType
ALU = mybir.AluOpType

S = 4096
D = 1024
NT = S // 128
EPS = 1e-6


def sl(s, n, st=1):
    return slice(s, s + (n - 1) * st + 1, st)


class Tok:
    __slots__ = ("lw", "rd")

    def __init__(self):
        self.lw = None
        self.rd = {}


def toks(n):
    return [Tok() for _ in range(n)]


class Op:
    __slots__ = ("issuer", "chan", "fn", "deps", "sig", "val", "is_dma", "inc")

    def __init__(self, issuer, chan, fn, is_dma, inc=None):
        self.inc = inc if inc is not None else (16 if is_dma else 1)
        self.issuer = issuer
        self.chan = chan
        self.fn = fn
        self.deps = set()
        self.sig = False
        self.val = 0
        self.is_dma = is_dma


class Prog:
    def __init__(self, nc, n_dma_ch=12):
        self.nc = nc
        self.ops = []
        self.last_on_chan = {}
        self.dma_rr = 0
        self.n_dma_ch = n_dma_ch
        self.bar_deps = set()

    def _add(self, issuer, chan, fn, reads, writes, is_dma, inc=None):
        op = Op(issuer, chan, fn, is_dma, inc)
        idx = len(self.ops)
        op.deps |= self.bar_deps
        for t in reads:
            if t.lw is not None:
                op.deps.add(t.lw)
        for t in writes:
            if t.lw is not None:
                w = self.ops[t.lw]
                if is_dma or w.chan != chan:
                    op.deps.add(t.lw)
            for c, r in t.rd.items():
                if is_dma or c != chan:
                    op.deps.add(r)
        if is_dma and chan in self.last_on_chan:
            op.deps.add(self.last_on_chan[chan])
        self.last_on_chan[chan] = idx
        for t in reads:
            t.rd[chan] = idx
        for t in writes:
            t.lw = idx
            t.rd = {}
        self.ops.append(op)
        return idx

    def op(self, eng, fn, reads=(), writes=()):
        return self._add(eng, eng, fn, reads, writes, False)

    def dma(self, issuer, fn, reads=(), writes=()):
        chan = "dma%d" % self.dma_rr
        self.dma_rr = (self.dma_rr + 1) % self.n_dma_ch
        return self._add(issuer, chan, fn, reads, writes, True)

    def cc(self, fn, reads=(), writes=()):
        return self._add("pool", "cc", fn, reads, writes, True, inc=1)

    def barrier(self):
        self.bar_deps = set(self.last_on_chan.values())

    def emit(self):
        nc = self.nc
        ops = self.ops
        for op in ops:
            for d in op.deps:
                ops[d].sig = True
        cnt = {}
        for op in ops:
            if op.is_dma:
                cnt[op.chan] = cnt.get(op.chan, 0) + op.inc
                op.val = cnt[op.chan]
                op.sig = True
            elif op.sig:
                cnt[op.chan] = cnt.get(op.chan, 0) + 1
                op.val = cnt[op.chan]
        chans = sorted(set(op.chan for op in ops))
        issuers = sorted(set(op.issuer for op in ops) | {"sp"})
        with contextlib.ExitStack() as es:
            sems = {c: es.enter_context(nc.semaphore("s_" + c)) for c in chans}
            block = es.enter_context(nc.Block())
            engmap = {"pe": block.tensor, "act": block.scalar, "dve": block.vector,
                      "pool": block.gpsimd, "sp": block.sync}
            final = {c: cnt.get(c, 0) for c in chans}

            def make(issuer):
                def body(e):
                    known = {}
                    for op in ops:
                        if op.issuer != issuer:
                            continue
                        need = {}
                        for d in op.deps:
                            dop = ops[d]
                            if dop.val > need.get(dop.chan, 0):
                                need[dop.chan] = dop.val
                        for c, v in need.items():
                            if known.get(c, 0) < v:
                                e.wait_ge(sems[c], v)
                                known[c] = v
                        ins = op.fn(e)
                        if op.sig:
                            ins.then_inc(sems[op.chan], op.inc)
                    if issuer == "sp":
                        for c, v in final.items():
                            if v > 0:
                                e.wait_ge(sems[c], v)
                return body

            for issuer in issuers:
                engmap[issuer](make(issuer))
        return nc


class Ctx:
    pass


class Arena:
    def __init__(self, nc, nbytes=212480):
        self.n = nbytes // 2
        self.t = nc.alloc_sbuf_tensor("arena", [128, self.n], BF16)
        self.cur = 0
        self.base = 0

    def alloc(self, name, shape, dtype):
        n = 1
        for d in shape[1:]:
            n *= d
        nb = n * (2 if dtype == F32 else 1)
        nb = (nb + 15) // 16 * 16
        assert self.cur + nb <= self.n, ("arena overflow", name, self.cur, nb, self.n)
        ap = self.t[:, self.cur:self.cur + nb]
        self.cur += nb
        if dtype == F32:
            ap = ap.bitcast(F32)
        ap = ap[:, 0:n]
        if len(shape) == 3:
            ap = ap.rearrange("p (a b) -> p a b", a=shape[1])
        elif len(shape) == 4:
            ap = ap.rearrange("p (a b c) -> p a b c", a=shape[1], b=shape[2])
        return ap

    def mark(self):
        self.base = self.cur

    def reset(self):
        self.cur = self.base


def common_setup(nc, P, names):
    c = Ctx()
    c.nc = nc
    c.P = P
    c.arena = Arena(nc)
    A = c.arena.alloc
    c.psall_t = nc.alloc_psum_tensor("psall", [128, 4096], F32)
    c.psall = c.psall_t
    c.ps = [c.psall_t[:, i * 512:(i + 1) * 512] for i in range(8)]
    c.psb = [c.psall_t.bitcast(BF16)[:, i * 1024:(i + 1) * 1024] for i in range(8)]
    c.t_ps_single = toks(8)
    c.t_ps = c.t_ps_single
    c.t_pspair = toks(4)
    c.ident_d = nc.dram_tensor("ident", [128, 128], BF16, kind="ExternalInput").ap()
    c.ident = A("ident_sb", [128, 128], BF16)
    c.t_ident = Tok()
    P.dma("sp", lambda e: e.dma_start(out=c.ident[:], in_=c.ident_d), writes=[c.t_ident])
    c.wstage = [A("wstage%d" % i, [128, 8, 128], F32) for i in range(2)]
    c.t_wstage = toks(2)
    c.wstage_i = 0
    c.arena.mark()
    return c


def phase1(c, hprev, p0, p1, nw, hout, hnT, t_hnT, nparts=2, scr=None, aux=None, ntiles=NT):
    nc, P = c.nc, c.P
    A = c.arena.alloc
    if scr is None:
        scr = A("p1scr", [128, 6 * D], F32)
    xt = [scr[:, i * D:(i + 1) * D] for i in range(2)]
    pa = [scr[:, (2 + i) * D:(3 + i) * D] for i in range(2)]
    pb = [scr[:, (4 + i) * D:(5 + i) * D] for i in range(2)]
    if aux is None:
        aux = A("p1aux", [128, 5 * D], BF16)
    hn = [aux[:, i * D:(i + 1) * D] for i in range(2)]
    sq = aux[:, 2 * D:3 * D]
    nwb = aux[:, 3 * D:5 * D].bitcast(F32)
    ssq = [A("p1ssq%d" % i, [128, 1], F32) for i in range(2)]
    rstd = [A("p1rstd%d" % i, [128, 1], F32) for i in range(2)]
    t_xt, t_pa, t_pb, t_hn, t_ssq, t_rstd = toks(2), toks(2), toks(2), toks(2), toks(2), toks(2)
    t_sq, t_nwb = Tok(), Tok()
    P.dma("sp", lambda e: e.dma_start(out=nwb[:], in_=nw.broadcast_to([128, D])), writes=[t_nwb])
    for t in range(ntiles):
        b = t % 2
        rows = slice(t * 128, (t + 1) * 128)
        P.dma("sp", lambda e, b=b, rows=rows: e.dma_start(out=xt[b][:], in_=hprev[rows, :]), writes=[t_xt[b]])
        if p0 is not None:
            P.dma("act", lambda e, b=b, rows=rows: e.dma_start(out=pa[b][:], in_=p0[rows, :]), writes=[t_pa[b]])
            if p1 is not None:
                P.dma("act", lambda e, b=b, rows=rows: e.dma_start(out=pb[b][:], in_=p1[rows, :]), writes=[t_pb[b]])
                P.op("pool", lambda e, b=b: e.tensor_tensor(out=pa[b][:], in0=pa[b][:], in1=pb[b][:], op=ALU.add),
                     reads=[t_pa[b], t_pb[b]], writes=[t_pa[b]])
            P.op("dve", lambda e, b=b: e.tensor_tensor(out=xt[b][:], in0=xt[b][:], in1=pa[b][:], op=ALU.add),
                 reads=[t_xt[b], t_pa[b]], writes=[t_xt[b]])
        if hout is not None and hnT is not None:
            P.dma("sp", lambda e, b=b, rows=rows: e.dma_start(out=hout[rows, :], in_=xt[b][:]), reads=[t_xt[b]])
        P.op("act", lambda e, b=b: e.activation(out=sq[:], in_=xt[b][:], func=AF.Square, accum_out=ssq[b][:]),
             reads=[t_xt[b]], writes=[t_sq, t_ssq[b]])
        P.op("act", lambda e, b=b: e.activation(out=rstd[b][:], in_=ssq[b][:], func=AF.Sqrt, scale=1.0 / D, bias=EPS),
             reads=[t_ssq[b]], writes=[t_rstd[b]])
        P.op("dve", lambda e, b=b: e.reciprocal(out=rstd[b][:], in_=rstd[b][:]), reads=[t_rstd[b]], writes=[t_rstd[b]])
        if hnT is None:
            P.op("dve", lambda e, b=b: e.scalar_tensor_tensor(out=pa[b][:], in0=xt[b][:], scalar=rstd[b][:], in1=nwb[:],
                                                             op0=ALU.mult, op1=ALU.mult),
                 reads=[t_xt[b], t_rstd[b], t_nwb], writes=[t_pa[b]])
            P.dma("sp", lambda e, b=b, rows=rows: e.dma_start(out=hout[rows, :], in_=pa[b][:]), reads=[t_pa[b]])
            continue
        P.op("dve", lambda e, b=b: e.scalar_tensor_tensor(out=hn[b][:], in0=xt[b][:], scalar=rstd[b][:], in1=nwb[:],
                                                         op0=ALU.mult, op1=ALU.mult),
             reads=[t_xt[b], t_rstd[b], t_nwb], writes=[t_hn[b]])
        k = t % 2
        for ch in range(8):
            P.op("pe", lambda e, b=b, ch=ch, k=k: e.transpose(out=c.psb[k][:, ch * 128:(ch + 1) * 128],
                                                             in_=hn[b][:, ch * 128:(ch + 1) * 128], identity=c.ident[:]),
                 reads=[t_hn[b], c.t_ident], writes=[c.t_ps[k]])
        P.op("act", lambda e, t=t, k=k: e.activation(out=hnT[:, :, t * 128:(t + 1) * 128],
                                                    in_=c.psb[k][:, :].rearrange("p (c n) -> p c n", c=8), func=AF.Copy),
             reads=[c.t_ps[k]], writes=[t_hnT])


def load_w(c, wdram, col0, ncols, dst, t_dst, row0=0, nch=8):
    for n0 in range(0, ncols, 128):
        k = c.wstage_i % 2
        c.wstage_i += 1
        st, t_st = c.wstage[k], c.t_wstage[k]
        c.P.dma("sp", lambda e, st=st, n0=n0: e.dma_start(
            out=st[:, 0:nch, :], in_=wdram[row0:row0 + nch * 128, col0 + n0:col0 + n0 + 128].rearrange("(c p) n -> p c n", p=128)),
            writes=[t_st])
        c.P.op("pool", lambda e, st=st, n0=n0: e.tensor_copy(out=dst[:, 0:nch, n0:n0 + 128], in_=st[:, 0:nch, :]),
               reads=[t_st], writes=[t_dst])


def proj_fm(c, wt, t_w, hnT, t_hnT, tb, bank, wtoks=()):
    for ch in range(8):
        c.P.op("pe", lambda e, ch=ch: e.matmul(c.ps[bank][:, :], lhsT=wt[:, ch, :], rhs=hnT[:, ch, tb * 512:(tb + 1) * 512],
                                              start=(ch == 0), stop=(ch == 7)),
               reads=[t_w, t_hnT], writes=[c.t_ps[bank], *wtoks])


def outproj(c, mixT, t_mixT, wout, pout, region, t_pout=None):
    nc, P = c.nc, c.P
    A = c.arena.alloc
    wo = region[:, 0:8192].rearrange("p (c n) -> p c n", c=8)
    t_wo = Tok()
    load_w(c, wout, 0, D, wo, t_wo)
    mt = [region[:, 8192 + i * 4096:8192 + (i + 1) * 4096].rearrange("p (c n) -> p c n", c=8) for i in range(2)]
    ot = [region[:, 16384 + i * 2048:16384 + (i + 1) * 2048].bitcast(F32) for i in range(2)]
    t_mt, t_ot = toks(2), toks(2)
    for tb in range(S // 512):
        mb = tb % 2
        P.dma("sp", lambda e, tb=tb, mb=mb: e.dma_start(out=mt[mb][:], in_=mixT[:, :, tb * 512:(tb + 1) * 512].rearrange("c p n -> p c n")),
              writes=[t_mt[mb]])
        for tt in range(4):
            t = tb * 4 + tt
            ob = t % 2
            for hf in range(2):
                bank = (2 * t + hf) % 4
                for ch in range(8):
                    P.op("pe", lambda e, ch=ch, hf=hf, bank=bank, mb=mb, tt=tt: e.matmul(
                        c.ps[bank][:, :], lhsT=mt[mb][:, ch, tt * 128:(tt + 1) * 128], rhs=wo[:, ch, hf * 512:(hf + 1) * 512],
                        start=(ch == 0), stop=(ch == 7)), reads=[t_mt[mb], t_wo], writes=[c.t_ps[bank]])
                if hf == 0:
                    P.op("act", lambda e, ob=ob, bank=bank: e.activation(out=ot[ob][:, 0:512], in_=c.ps[bank][:, :], func=AF.Copy),
                         reads=[c.t_ps[bank]], writes=[t_ot[ob]])
                else:
                    P.op("dve", lambda e, ob=ob, bank=bank: e.tensor_copy(out=ot[ob][:, 512:1024], in_=c.ps[bank][:, :]),
                         reads=[c.t_ps[bank]], writes=[t_ot[ob]])
            P.dma("sp", lambda e, t=t, ob=ob: e.dma_start(out=pout[t * 128:(t + 1) * 128, :], in_=ot[ob][:]),
                  reads=[t_ot[ob]] + ([t_pout[t // 8]] if t_pout is not None else []))


A_PATTERNS = (1, 4, 16)


def build_even(stage=99, npairs=4, patterns=(1, 4, 16), nheadsB=4, skip=(), dbg=False, fused=None):
    if fused is None:
        nc = bass.Bass("TRN2", target_bir_lowering=False)
        P = Prog(nc)
        pre = ""
    else:
        nc, P, pre = fused["nc"], fused["P"], fused["pre"]
    dram = lambda n, shp, dt=F32: nc.dram_tensor(pre + n, shp, dt, kind="ExternalInput").ap()
    if fused is None:
        hprev = dram("hprev", [S, D])
        p0 = dram("p0", [S, D])
        p1 = dram("p1", [S, D])
    else:
        hprev, p0, p1 = fused["hprev"], fused["psum"], None
    nw = dram("nw", [1, D])
    win = dram("win", [D, 4096])
    wout = dram("wout", [D, D])
    lbl = dram("lbl", [128, 8])
    lbsel = dram("lbsel", [128, 1])
    hnw = dram("hnw", [128, 1])
    maskA_d = dram("maskA", [128, 512], BF16)
    onese_d = dram("onese", [128, 256], BF16)
    maskB_d = dram("maskB", [128, 128], BF16)
    rowmask_d = dram("rowmask", [128, 2])
    resetm_d = dram("resetm", [128, 512])
    if fused is None:
        hout = nc.dram_tensor("hout", [S, D], F32, kind="ExternalOutput").ap()
        pout = nc.dram_tensor("pout", [S, D], F32, kind="ExternalOutput").ap()
        mixT = nc.dram_tensor("mixT", [8, 128, S], BF16, kind=("ExternalOutput" if dbg else "Internal")).ap()
        c = common_setup(nc, P, None)
    else:
        hout, pout, mixT, c = fused["hout"], fused["pout"], fused["mixT"], fused["c"]
        c.arena.reset()
    c.t_ps = c.t_ps_single
    t_mixT = Tok()
    A = c.arena.alloc
    hnT = A("hnT", [128, 8, S], BF16)
    t_hnT = Tok()

    scr = A("scr", [128, 2 * S], F32)
    phase1(c, hprev, p0, p1, nw, hout, hnT, t_hnT, scr=scr)
    P.barrier()
    if stage <= 1:
        P.emit()
        return nc

    maskA = A("maskA_sb", [128, 512], BF16)
    onese = A("onese_sb", [128, 2, 128], BF16)
    maskB = A("maskB_sb", [128, 128], BF16)
    rowmask = A("rowmask_sb", [128, 2], F32)
    resetm = A("resetm_sb", [128, 512], F32)
    onesb = A("onesb", [128, 128], BF16)
    lbt = A("lbt", [128, 8], F32)
    lbs = A("lbs", [128, 1], F32)
    lb = A("lb", [128, 4], F32)
    oml = A("oml", [128, 4], F32)
    hnws = A("hnws", [128, 1], F32)
    t_const = Tok()
    for dst, src in ((maskA[:], maskA_d), (onese[:], onese_d.rearrange("p (a b) -> p a b", a=2)), (maskB[:], maskB_d),
                     (resetm[:], resetm_d), (rowmask[:], rowmask_d), (lbt[:], lbl), (lbs[:], lbsel), (hnws[:], hnw)):
        P.dma("sp", lambda e, dst=dst, src=src: e.dma_start(out=dst, in_=src), writes=[t_const])
    P.op("pool", lambda e: e.memset(onesb[:], 1.0), writes=[t_const])
    P.op("dve", lambda e: e.tensor_tensor(out=lb[:], in0=lbt[:, 4:8], in1=lbt[:, 0:4], op=ALU.subtract), reads=[t_const], writes=[t_const])
    P.op("act", lambda e: e.activation(out=lb[:], in_=lb[:], func=AF.Sigmoid), reads=[t_const], writes=[t_const])
    P.op("dve", lambda e: e.tensor_scalar(out=lb[:], in0=lb[:], scalar1=lbs[:], scalar2=None, op0=ALU.mult), reads=[t_const], writes=[t_const])
    P.op("dve", lambda e: e.tensor_scalar(out=oml[:], in0=lb[:], scalar1=-1.0, scalar2=1.0, op0=ALU.mult, op1=ALU.add),
         reads=[t_const], writes=[t_const])

    wq = [A("wq%d" % i, [128, 8, 128], BF16) for i in range(2)]
    wk = [A("wk%d" % i, [128, 8, 128], BF16) for i in range(2)]
    wv = [A("wv%d" % i, [128, 8, 128], BF16) for i in range(2)]
    wg = [A("wg%d" % i, [128, 8, 128], BF16) for i in range(2)]
    t_wq, t_wk, t_wv, t_wg = toks(2), toks(2), toks(2), toks(2)
    regA = A("regA", [128, 5 * S], BF16)
    QT, KT, GT = regA[:, 0:S], regA[:, S:2 * S], regA[:, 2 * S:3 * S]
    QTo = regA[:, 4 * S:5 * S]
    QTs = (QT, QTo)
    t_QT, t_KT, t_GT = Tok(), Tok(), Tok()
    acc = scr[:, :].rearrange("p (a n) -> p a n", a=2)
    t_acc = Tok()
    NV = 4
    NPT = 4
    Vz = [A("Vz%d" % i, [128, 2, 128], BF16) for i in range(NV)]
    t_Vz = toks(NV)
    PT = [A("PT%d" % i, [128, 2, 256], BF16) for i in range(NPT)]
    t_PT = toks(NPT)
    mixA = regA[:, 3 * S:4 * S]
    t_mixA = Tok()
    for i in range(NV):
        P.op("pool", lambda e, i=i: e.memset(Vz[i][:], 0.0), writes=[t_Vz[i]])

    kb_count = 0
    for p in range(npairs):
        wb = p % 2
        load_w(c, win, 0 * 512 + p * 128, 128, wq[wb], t_wq[wb])
        load_w(c, win, 1 * 512 + p * 128, 128, wk[wb], t_wk[wb])
        load_w(c, win, 2 * 512 + p * 128, 128, wv[wb], t_wv[wb])
        load_w(c, win, 3 * 512 + p * 128, 128, wg[wb], t_wg[wb])
        for tb in range(8):
            cols = slice(tb * 512, (tb + 1) * 512)
            bq, bk, bg = (3 * tb) % 4, (3 * tb + 1) % 4, (3 * tb + 2) % 4
            proj_fm(c, wq[wb], t_wq[wb], hnT, t_hnT, tb, bq)
            P.op("act", lambda e, cols=cols, bq=bq: e.activation(out=QT[:, cols], in_=c.ps[bq][:, :], func=AF.Copy, scale=rowmask[:, 0:1]),
                 reads=[c.t_ps[bq], t_const], writes=[t_QT])
            P.op("dve", lambda e, cols=cols, bq=bq: e.tensor_scalar(out=QTo[:, cols], in0=c.ps[bq][:, :], scalar1=rowmask[:, 1:2], scalar2=None,
                                                                   op0=ALU.mult), reads=[c.t_ps[bq], t_const], writes=[t_QT])
            proj_fm(c, wk[wb], t_wk[wb], hnT, t_hnT, tb, bk)
            P.op("dve", lambda e, cols=cols, bk=bk: e.tensor_copy(out=KT[:, cols], in_=c.ps[bk][:, :]),
                 reads=[c.t_ps[bk]], writes=[t_KT])
            proj_fm(c, wg[wb], t_wg[wb], hnT, t_hnT, tb, bg)
            P.op("act", lambda e, cols=cols, bg=bg: e.activation(out=GT[:, cols], in_=c.ps[bg][:, :], func=AF.Silu),
                 reads=[c.t_ps[bg]], writes=[t_GT])
        P.op("pool", lambda e: e.memset(acc[:], 0.0), writes=[t_acc])
        for d in (() if 'attn' in skip else patterns):
            nb = S // (128 * d)
            for r in range(d):
                for n in range(nb):
                    nq = 256 if n < nb - 1 else 128
                    k0 = 128 * n * d + r
                    keys = sl(k0, 128, d)
                    qs = sl(k0, nq, d)
                    vi = kb_count % NV
                    pi = kb_count % NPT
                    bv = (4, 0)[kb_count % 2]
                    bs = (5, 6, 1)[kb_count % 3]
                    bn = (7, 3, 2)[kb_count % 3]
                    kb_count += 1
                    for ch in range(8):
                        P.op("pe", lambda e, ch=ch, keys=keys, wb=wb, bv=bv: e.matmul(
                            c.ps[bv][:, 0:128], lhsT=hnT[:, ch, keys], rhs=wv[wb][:, ch, :], start=(ch == 0), stop=(ch == 7)),
                            reads=[t_hnT, t_wv[wb]], writes=[c.t_ps[bv]])
                    P.op("dve", lambda e, vi=vi, bv=bv: e.tensor_copy(out=Vz[vi][:, 0, 0:64], in_=c.ps[bv][:, 0:64]),
                         reads=[c.t_ps[bv]], writes=[t_Vz[vi]])
                    P.op("dve", lambda e, vi=vi, bv=bv: e.tensor_copy(out=Vz[vi][:, 1, 64:128], in_=c.ps[bv][:, 64:128]),
                         reads=[c.t_ps[bv]], writes=[t_Vz[vi]])
                    st3 = c.ps[bs][:, :].rearrange("p (a n) -> p a n", a=2)
                    for hh in range(2):
                        P.op("pe", lambda e, hh=hh, keys=keys, qs=qs, nq=nq, st3=st3: e.matmul(
                            st3[:, hh, 0:nq], lhsT=KT[:, keys], rhs=QTs[hh][:, qs], start=True, stop=True),
                            reads=[t_KT, t_QT], writes=[c.t_ps[bs]])
                    P.op("act", lambda e, pi=pi, nq=nq, st3=st3: e.activation(out=PT[pi][:, :, 0:nq], in_=st3[:, :, 0:nq],
                                                                             func=AF.Exp, scale=0.125),
                         reads=[c.t_ps[bs]], writes=[t_PT[pi]])
                    m3 = maskA[:, :].rearrange("p (a n) -> p a n", a=2)
                    P.op("pool", lambda e, pi=pi, nq=nq, m3=m3: e.tensor_tensor(out=PT[pi][:, :, 0:nq], in0=PT[pi][:, :, 0:nq],
                                                                               in1=m3[:, :, 0:nq], op=ALU.mult),
                         reads=[t_PT[pi], t_const], writes=[t_PT[pi]])
                    nd3 = c.ps[bn][:, :].rearrange("p (a n) -> p a n", a=2)
                    for which, lh in ((0, Vz[vi]), (1, onese)):
                        for hh in range(2):
                            P.op("pe", lambda e, which=which, lh=lh, hh=hh, pi=pi, nq=nq, nd3=nd3: e.matmul(
                                nd3[:, which, 0:nq], lhsT=lh[:, hh, :], rhs=PT[pi][:, hh, 0:nq], start=(hh == 0), stop=(hh == 1)),
                                reads=[t_Vz[vi], t_const, t_PT[pi]], writes=[c.t_ps[bn]])
                    P.op("dve", lambda e, qs=qs, nq=nq, nd3=nd3: e.tensor_tensor(out=acc[:, :, qs], in0=acc[:, :, qs],
                                                                                in1=nd3[:, :, 0:nq], op=ALU.add),
                         reads=[c.t_ps[bn], t_acc], writes=[t_acc])
        for q4 in (() if 'fin' in skip else range(4)):
            cols = slice(q4 * 1024, (q4 + 1) * 1024)
            P.op("dve", lambda e, cols=cols: e.reciprocal(out=acc[:, 1, cols], in_=acc[:, 1, cols]), reads=[t_acc], writes=[t_acc])
            P.op("dve", lambda e, cols=cols: e.tensor_tensor(out=acc[:, 0, cols], in0=acc[:, 0, cols], in1=acc[:, 1, cols], op=ALU.mult),
                 reads=[t_acc], writes=[t_acc])
            P.op("pool", lambda e, cols=cols: e.tensor_tensor(out=mixA[:, cols], in0=acc[:, 0, cols], in1=GT[:, cols], op=ALU.mult),
                 reads=[t_acc, t_GT], writes=[t_mixA])
        if 'mixdma' not in skip:
            P.dma("sp", lambda e, p=p: e.dma_start(out=mixT[p, :, :], in_=mixA[:, :]), reads=[t_mixA])

    P.barrier()
    if stage <= 2:
        nheadsB = 0
    wbq, wbf, wbi, wbg = wq, wk, wv, wg
    t_wbq, t_wbf, t_wbi, t_wbg = t_wq, t_wk, t_wv, t_wg
    _cur = [0]

    def carve(nbf16):
        a = regA[:, _cur[0]:_cur[0] + nbf16]
        _cur[0] += nbf16
        return a
    f32t = lambda n: carve(1024).bitcast(F32)
    sig, kTt, lf, Gc, Gx, ex = f32t("b_sig"), f32t("b_kT"), f32t("b_lf"), f32t("b_Gc"), f32t("b_Gx"), f32t("b_ex")
    t_sig, t_kT, t_lf, t_Gc, t_Gx, t_ex = toks(6)
    ex2, ex3, ex4, Gx2 = f32t("b_ex2"), f32t("b_ex3"), f32t("b_ex4"), f32t("b_Gx2")
    t_ex2, t_ex3, t_ex4, t_Gx2 = toks(4)
    bft = lambda n: carve(512)
    GTb, qt, kt, qd, ktl, sqb, mixB = bft("b_GT"), bft("b_qt"), bft("b_kt"), bft("b_qd"), bft("b_ktl"), bft("b_sq"), bft("b_mix")
    t_GTb, t_qt, t_kt, t_qd, t_ktl, t_sqb, t_mixB = toks(7)
    eg = A("b_eg", [128, 8], F32)
    t_eg = Tok()
    kt_tok = A("b_kttok", [128, 2, 4, 128], BF16)
    v_tok = A("b_vtok", [128, 4, 128], BF16)
    t_kttok, t_vtok = Tok(), Tok()
    at = A("b_at", [128, 128], BF16)
    t_at = Tok()
    St2 = [A("b_S0", [128, 128], F32), A("b_S1", [128, 128], F32)]
    St = St2[0]
    Sb = A("b_Sb", [128, 128], BF16)
    t_S, t_Sb = Tok(), Tok()
    rs = carve(1024).bitcast(F32)
    t_rs = Tok()
    Gc3 = Gc[:, :].rearrange("p (c n) -> p c n", n=64)
    for hb in range(nheadsB):
        wb = hb % 2
        load_w(c, win, 4 * 512 + hb * 128, 128, wbq[wb], t_wbq[wb])
        load_w(c, win, 5 * 512 + hb * 128, 128, wbf[wb], t_wbf[wb])
        load_w(c, win, 6 * 512 + hb * 128, 128, wbi[wb], t_wbi[wb])
        load_w(c, win, 7 * 512 + hb * 128, 128, wbg[wb], t_wbg[wb])
        P.op("pool", lambda e: e.memset(St2[0][:], 0.0), writes=[t_S])
        P.op("pool", lambda e: e.memset(St2[1][:], 0.0), writes=[t_S])
        s_cur = 0
        P.op("pool", lambda e: e.memset(Sb[:], 0.0), writes=[t_Sb])
        for tb in range(8):
            BQ, BF_, BG, BV, BA, BD, BO, BS = 0, 1, 2, 3, 4, 5, 6, 7
            proj_fm(c, wbq[wb], t_wbq[wb], hnT, t_hnT, tb, BQ)
            proj_fm(c, wbf[wb], t_wbf[wb], hnT, t_hnT, tb, BF_)
            proj_fm(c, wbg[wb], t_wbg[wb], hnT, t_hnT, tb, BG)
            P.op("act", lambda e: e.activation(out=GTb[:], in_=c.ps[BG][:, :], func=AF.Silu), reads=[c.t_ps[BG]], writes=[t_GTb])
            P.op("act", lambda e: e.activation(out=sig[:], in_=c.ps[BF_][:, :], func=AF.Sigmoid), reads=[c.t_ps[BF_]], writes=[t_sig])
            P.op("dve", lambda e, hb=hb: e.tensor_scalar(out=sig[:], in0=sig[:], scalar1=oml[:, hb:hb + 1], scalar2=lb[:, hb:hb + 1],
                                                        op0=ALU.mult, op1=ALU.add), reads=[t_sig, t_const], writes=[t_sig])
            P.op("pool", lambda e: e.tensor_scalar(out=kTt[:], in0=sig[:], scalar1=-1.0, scalar2=1.0, op0=ALU.mult, op1=ALU.add),
                 reads=[t_sig], writes=[t_kT])
            P.op("act", lambda e: e.activation(out=lf[:], in_=sig[:], func=AF.Ln), reads=[t_sig], writes=[t_lf])
            P.op("dve", lambda e: e.tensor_tensor_scan(out=Gc[:], data0=resetm[:], data1=lf[:], initial=0.0, op0=ALU.mult, op1=ALU.add),
                 reads=[t_lf, t_const], writes=[t_Gc])
            P.op("dve", lambda e: e.tensor_tensor(out=Gx[:, :].rearrange("p (c n) -> p c n", n=64), in0=Gc3,
                                                 in1=Gc3[:, :, 31:32].broadcast_to([128, 8, 64]), op=ALU.subtract),
                 reads=[t_Gc], writes=[t_Gx])
            P.op("dve", lambda e: e.tensor_scalar(out=Gx[:], in0=Gx[:], scalar1=-40.0, scalar2=40.0, op0=ALU.max, op1=ALU.min),
                 reads=[t_Gx], writes=[t_Gx])
            P.op("act", lambda e: e.activation(out=ex[:], in_=Gx[:], func=AF.Exp), reads=[t_Gx], writes=[t_ex])
            P.op("dve", lambda e: e.tensor_tensor(out=qt[:], in0=c.ps[BQ][:, :], in1=ex[:], op=ALU.mult),
                 reads=[c.t_ps[BQ], t_ex], writes=[t_qt])
            P.op("act", lambda e: e.activation(out=ex2[:], in_=Gx[:], func=AF.Exp, scale=-1.0), reads=[t_Gx], writes=[t_ex2])
            P.op("pool", lambda e: e.tensor_tensor(out=kt[:], in0=kTt[:], in1=ex2[:], op=ALU.mult), reads=[t_kT, t_ex2], writes=[t_kt])
            P.op("act", lambda e: e.activation(out=ex3[:], in_=Gc[:], func=AF.Exp), reads=[t_Gc], writes=[t_ex3])
            P.op("dve", lambda e: e.tensor_tensor(out=qd[:], in0=c.ps[BQ][:, :], in1=ex3[:], op=ALU.mult),
                 reads=[c.t_ps[BQ], t_ex3], writes=[t_qd])
            P.op("dve", lambda e: e.tensor_tensor(out=Gx2[:, :].rearrange("p (c n) -> p c n", n=64),
                                                 in0=Gc3[:, :, 63:64].broadcast_to([128, 8, 64]), in1=Gc3, op=ALU.subtract),
                 reads=[t_Gc], writes=[t_Gx2])
            P.op("act", lambda e: e.activation(out=ex4[:], in_=Gx2[:], func=AF.Exp), reads=[t_Gx2], writes=[t_ex4])
            P.op("pool", lambda e: e.tensor_tensor(out=ktl[:], in0=kTt[:], in1=ex4[:], op=ALU.mult), reads=[t_kT, t_ex4], writes=[t_ktl])
            P.op("act", lambda e: e.activation(out=eg[:], in_=Gc[:, sl(63, 8, 64)], func=AF.Exp), reads=[t_Gc], writes=[t_eg])
            for tt in range(4):
                P.op("pe", lambda e, tt=tt: e.transpose(out=c.psb[BV][:, tt * 128:(tt + 1) * 128], in_=ktl[:, tt * 128:(tt + 1) * 128],
                                                       identity=c.ident[:]), reads=[t_ktl, c.t_ident], writes=[c.t_ps[BV]])
            P.op("act", lambda e: e.activation(out=kt_tok[:, 0, :, :], in_=c.psb[BV][:, 0:512].rearrange("p (a n) -> p a n", a=4), func=AF.Copy,
                                               scale=rowmask[:, 0:1]), reads=[c.t_ps[BV], t_const], writes=[t_kttok])
            P.op("dve", lambda e: e.tensor_scalar(out=kt_tok[:, 1, :, :], in0=c.psb[BV][:, 0:512].rearrange("p (a n) -> p a n", a=4),
                                                 scalar1=rowmask[:, 1:2], scalar2=None, op0=ALU.mult),
                 reads=[c.t_ps[BV], t_const], writes=[t_kttok])
            for tt in range(4):
                tk = slice(tb * 512 + tt * 128, tb * 512 + (tt + 1) * 128)
                for ch in range(8):
                    P.op("pe", lambda e, tt=tt, tk=tk, ch=ch, wb=wb: e.matmul(c.ps[BV][:, tt * 128:(tt + 1) * 128], lhsT=hnT[:, ch, tk],
                                                                             rhs=wbi[wb][:, ch, :], start=(ch == 0), stop=(ch == 7)),
                         reads=[t_hnT, t_wbi[wb]], writes=[c.t_ps[BV]])
            P.op("act", lambda e: e.activation(out=v_tok[:, :, :], in_=c.ps[BV][:, :].rearrange("p (a n) -> p a n", a=4), func=AF.Copy),
                 reads=[c.t_ps[BV]], writes=[t_vtok])
            for tt in range(4):
                tk = slice(tt * 128, (tt + 1) * 128)
                P.op("pe", lambda e, tk=tk: e.matmul(c.ps[BA][:, 0:128], lhsT=kt[:, tk], rhs=qt[:, tk], start=True, stop=True),
                     reads=[t_kt, t_qt], writes=[c.t_ps[BA]])
                P.op("dve", lambda e: e.tensor_tensor(out=at[:, :], in0=c.ps[BA][:, 0:128], in1=maskB[:, :], op=ALU.mult),
                     reads=[c.t_ps[BA], t_const], writes=[t_at])
                for cc in range(2):
                    rr = slice(cc * 64, (cc + 1) * 64)
                    oc = slice(tt * 128 + cc * 64, tt * 128 + (cc + 1) * 64)
                    P.op("pe", lambda e, cc=cc, oc=oc, tt=tt: e.matmul(c.ps[BO][:, oc], lhsT=v_tok[:, tt, :], rhs=at[:, cc * 64:(cc + 1) * 64],
                                                                      start=True, stop=False),
                         reads=[t_vtok, t_at], writes=[c.t_ps[BO]])
                    P.op("pe", lambda e, oc=oc: e.matmul(c.ps[BO][:, oc], lhsT=Sb[:, :], rhs=qd[:, oc], start=False, stop=True),
                         reads=[t_Sb, t_qd], writes=[c.t_ps[BO]])
                    P.op("pe", lambda e, cc=cc, tt=tt: e.matmul(c.ps[BD][:, 0:128], lhsT=kt_tok[:, cc, tt, :], rhs=v_tok[:, tt, :],
                                                               start=True, stop=True),
                         reads=[t_kttok, t_vtok], writes=[c.t_ps[BD]])
                    ci = tt * 2 + cc
                    So, Sn = St2[s_cur], St2[1 - s_cur]
                    s_cur = 1 - s_cur
                    P.op("dve", lambda e, ci=ci, So=So: e.scalar_tensor_tensor(out=Sb[:], in0=So[:], scalar=eg[:, ci:ci + 1], in1=c.ps[BD][:, 0:128],
                                                                              op0=ALU.mult, op1=ALU.add),
                         reads=[t_S, t_eg, c.t_ps[BD]], writes=[t_Sb])
                    P.op("dve", lambda e, ci=ci, So=So, Sn=Sn: e.scalar_tensor_tensor(out=Sn[:], in0=So[:], scalar=eg[:, ci:ci + 1], in1=c.ps[BD][:, 0:128],
                                                                                     op0=ALU.mult, op1=ALU.add),
                         reads=[t_S, t_eg, c.t_ps[BD]], writes=[t_S])
            P.op("act", lambda e: e.activation(out=sqb[:], in_=c.ps[BO][:, :], func=AF.Square), reads=[c.t_ps[BO]], writes=[t_sqb])
            P.op("pe", lambda e: e.matmul(c.ps[BS][:, :], lhsT=onesb[:, :], rhs=sqb[:], start=True, stop=True),
                 reads=[t_sqb, t_const], writes=[c.t_ps[BS]])
            P.op("act", lambda e: e.activation(out=rs[:], in_=c.ps[BS][:, :], func=AF.Sqrt, scale=1.0 / 128, bias=EPS),
                 reads=[c.t_ps[BS]], writes=[t_rs])
            P.op("dve", lambda e: e.reciprocal(out=rs[:], in_=rs[:]), reads=[t_rs], writes=[t_rs])
            P.op("dve", lambda e: e.tensor_tensor(out=rs[:], in0=c.ps[BO][:, :], in1=rs[:], op=ALU.mult),
                 reads=[c.t_ps[BO], t_rs], writes=[t_rs])
            P.op("dve", lambda e: e.scalar_tensor_tensor(out=mixB[:], in0=rs[:], scalar=hnws[:, 0:1], in1=GTb[:], op0=ALU.mult, op1=ALU.mult),
                 reads=[t_rs, t_GTb, t_const], writes=[t_mixB])
            P.dma("sp", lambda e, hb=hb, tb=tb: e.dma_start(out=mixT[4 + hb, :, tb * 512:(tb + 1) * 512], in_=mixB[:, :]),
                  reads=[t_mixB])

    if 'outproj' in skip:
        P.emit()
        return nc
    if stage < 99:
        for k in list(range(npairs, 4)) + list(range(4 + nheadsB, 8)):
            P.dma("sp", lambda e, k=k: e.dma_start(out=mixT[k, :, :], in_=mixA[:, :]))
    P.barrier()
    outproj(c, mixT, t_mixT, wout, pout, hnT[:, :, :].rearrange("p c n -> p (c n)"), t_pout=(fused or {}).get("t_pout"))
    if fused is None:
        P.emit()
    return nc


def even_consts():
    bf = ml_dtypes.bfloat16
    k = np.arange(128)[:, None]
    q = np.arange(128)[None, :]
    half = np.concatenate([(q >= k), (k >= q)], axis=1).astype(np.float32)
    maskA = np.concatenate([half, half], axis=1).astype(bf)
    onese = np.zeros((128, 2, 128), np.float32)
    onese[:, 0, 0:64] = 1.0
    onese[:, 1, 64:128] = 1.0
    j = np.arange(64)[:, None]
    i = np.arange(64)[None, :]
    mb = (i >= j).astype(np.float32)
    maskB = np.zeros((128, 128), np.float32)
    maskB[0:64, 0:64] = mb
    maskB[64:128, 64:128] = mb
    maskB = maskB.astype(bf)
    rowmask = np.zeros((128, 2), np.float32)
    rowmask[0:64, 0] = 1.0
    rowmask[64:128, 1] = 1.0
    resetm = np.ones((128, 512), np.float32)
    resetm[:, ::64] = 0.0
    return {"maskA": maskA, "onese": onese.reshape(128, 256).astype(bf), "maskB": maskB, "resetm": resetm, "rowmask": rowmask,
            "ident": np.eye(128, dtype=np.float32).astype(bf)}


def even_inputs(hprev, p0, p1, norm_w_l, w_in, w_out, lb_logits, hgrn_nw, jl, j):
    A0, B0 = 0, 4096
    cols = []
    for blk in range(4):
        cols.append(w_in[:, A0 + blk * 1024 + j * 512: A0 + blk * 1024 + (j + 1) * 512])
    for blk in range(4):
        cols.append(w_in[:, B0 + blk * 1024 + j * 512: B0 + blk * 1024 + (j + 1) * 512])
    win = np.ascontiguousarray(np.concatenate(cols, axis=1))
    wout = np.ascontiguousarray(np.concatenate([w_out[j * 512:(j + 1) * 512], w_out[1024 + j * 512:1024 + (j + 1) * 512]], axis=0))
    lbl = lb_logits[:, j * 512:(j + 1) * 512].reshape(2, 4, 128).transpose(2, 0, 1).reshape(128, 8)
    m = {"hprev": hprev, "p0": p0, "p1": p1, "nw": norm_w_l.reshape(1, D), "win": win, "wout": wout,
         "lbl": np.ascontiguousarray(lbl), "lbsel": np.full((128, 1), float(jl), np.float32),
         "hnw": np.ascontiguousarray(hgrn_nw.reshape(128, 1))}
    m.update(even_consts())
    return m


def build_odd(nblocks=8, stage=99, dbg=False, fused=None):
    if fused is None:
        nc = bass.Bass("TRN2", target_bir_lowering=False)
        P = Prog(nc)
        pre = ""
    else:
        nc, P, pre = fused["nc"], fused["P"], fused["pre"]
    dram = lambda n, shp, dt=F32: nc.dram_tensor(pre + n, shp, dt, kind="ExternalInput").ap()
    if fused is None:
        hprev = dram("hprev", [S, D])
        p0 = dram("p0", [S, D])
        p1 = dram("p1", [S, D])
    else:
        hprev, p0, p1 = fused["hprev"], fused["psum"], None
    nw = dram("nw", [1, D])
    win = dram("win", [D, 3072 + 128])
    wout = dram("wout", [D, D])
    cw_d = dram("cw", [128, 64])
    hc_d = dram("hconst", [1, 16])
    onw_d = dram("onw", [128, 1])
    tri_d = dram("tri", [128, 384])
    sel_d = dram("csel", [128, 256])
    mstrict_d = dram("mstrict", [128, 128], BF16)
    mcausal_d = dram("mcausal", [128, 128], BF16)
    delta_d = dram("delta", [128, 8])
    rowmask_d = dram("rowmask", [128, 2])
    if fused is None:
        hout = nc.dram_tensor("hout", [S, D], F32, kind="ExternalOutput").ap()
        pout = nc.dram_tensor("pout", [S, D], F32, kind="ExternalOutput").ap()
        mixT = nc.dram_tensor("mixT", [8, 128, S], BF16, kind=("ExternalOutput" if dbg else "Internal")).ap()
        c = common_setup(nc, P, None)
    else:
        hout, pout, mixT, c = fused["hout"], fused["pout"], fused["mixT"], fused["c"]
        c.arena.reset()
    A = c.arena.alloc
    hnT = A("hnT", [128, 8, S], BF16)
    t_hnT = Tok()
    scr = A("scr", [128, 6 * D], F32)
    c.t_ps = [c.t_pspair[k // 2] for k in range(8)]
    xflat = A("xbuf", [128, 16 * 516], BF16)
    xbuf = xflat[:, :].rearrange("p (a n) -> p a n", a=16)
    phase1(c, hprev, p0, p1, nw, hout, hnT, t_hnT, scr=scr, aux=xflat[:, :])
    P.barrier()
    if stage <= 1:
        P.emit()
        return nc

    t_const = Tok()
    cw = A("cw_sb", [128, 64], F32)
    hcb = A("hcb", [128, 16], F32)
    onw = A("onw_sb", [128, 1], F32)
    tri = A("tri_sb", [128, 256], F32)
    csel = A("csel_sb", [128, 256], F32)
    mstrict = A("mstrict_sb", [128, 128], BF16)
    mcausal = A("mcausal_sb", [128, 128], BF16)
    delta = A("delta_sb", [128, 8], F32)
    rowmask = A("rowmask_sb", [128, 2], F32)
    onesb = A("onesb", [128, 128], BF16)
    onesf = A("onesf", [128, 128], F32)
    identf = A("identf", [128, 128], F32)
    negA = A("negA", [128, 8], F32)
    dg = A("dg", [128, 64, 128], BF16)
    for dst, src in ((cw[:], cw_d), (hcb[:], hc_d.broadcast_to([128, 16])), (onw[:], onw_d), (tri[:], tri_d[:, 0:256]),
                     (csel[:], sel_d), (mstrict[:], mstrict_d), (mcausal[:], mcausal_d), (delta[:], delta_d), (rowmask[:], rowmask_d)):
        P.dma("sp", lambda e, dst=dst, src=src: e.dma_start(out=dst, in_=src), writes=[t_const])
    P.op("pool", lambda e: e.memset(onesb[:], 1.0), writes=[t_const])
    P.op("pool", lambda e: e.memset(onesf[:], 1.0), writes=[t_const])
    P.op("dve", lambda e: e.tensor_copy(out=identf[:], in_=c.ident[:]), reads=[c.t_ident], writes=[t_const])
    P.op("act", lambda e: e.activation(out=negA[:], in_=hcb[:, 8:16], func=AF.Exp), reads=[t_const], writes=[t_const])
    P.op("dve", lambda e: e.tensor_scalar(out=negA[:], in0=negA[:], scalar1=-1.0, scalar2=None, op0=ALU.mult), reads=[t_const], writes=[t_const])
    for k in range(64):
        eng = "dve" if k % 2 == 0 else "pool"
        P.op(eng, lambda e, k=k: e.tensor_scalar(out=dg[:, k, :], in0=identf[:], scalar1=cw[:, k:k + 1], scalar2=None, op0=ALU.mult),
             reads=[t_const], writes=[t_const])

    wba = A("wba", [128, 8, 128], BF16)
    t_wba = Tok()
    load_w(c, win, 3072, 128, wba, t_wba)
    beta = A("beta", [128, 1, 8], F32)
    gpad = A("gpad", [128, 128], F32)
    t_beta, t_gpad = Tok(), Tok()
    xa = A("xa", [128, 8], F32)
    t_xa = Tok()
    gc = A("gc", [128, 1, 8], F32)
    gam = A("gam", [128, 1, 8], F32)
    bgam = A("bgam", [128, 1, 8], F32)
    etl = A("etl", [128, 1, 2, 8], F32)
    egl = A("egl", [128, 1, 2, 8], F32)
    gcT = A("gcT", [128, 1, 128], F32)
    t_sc = Tok()
    P.op("pool", lambda e: e.memset(gpad[:], 0.0), writes=[t_gpad])
    def scalars(t):
        tk = slice(t * 128, (t + 1) * 128)
        for ch in range(8):
            P.op("pe", lambda e, ch=ch, tk=tk: e.matmul(c.ps[0][:, 0:128], lhsT=hnT[:, ch, tk], rhs=wba[:, ch, :], start=(ch == 0), stop=(ch == 7)),
                 reads=[t_hnT, t_wba], writes=[c.t_ps[0]])
        P.op("act", lambda e, t=t: e.activation(out=beta[:, 0, :], in_=c.ps[0][:, 0:8], func=AF.Sigmoid), reads=[c.t_ps[0]], writes=[t_beta])
        P.op("dve", lambda e: e.tensor_tensor(out=xa[:], in0=c.ps[0][:, 8:16], in1=hcb[:, 0:8], op=ALU.add),
             reads=[c.t_ps[0], t_const], writes=[t_xa])
        P.op("act", lambda e: e.activation(out=xa[:], in_=xa[:], func=AF.Exp), reads=[t_xa], writes=[t_xa])
        P.op("act", lambda e: e.activation(out=xa[:], in_=xa[:], func=AF.Ln, bias=1.0), reads=[t_xa], writes=[t_xa])
        P.op("dve", lambda e: e.tensor_tensor(out=gpad[:, 0:8], in0=xa[:], in1=negA[:], op=ALU.mult), reads=[t_xa, t_const], writes=[t_gpad])
        P.op("pe", lambda e: e.matmul(c.ps[1][:, 0:8], lhsT=tri[:, 0:128], rhs=gpad[:, 0:8], start=True, stop=True),
             reads=[t_gpad, t_const], writes=[c.t_ps[1]])
        P.op("pe", lambda e: e.matmul(c.ps[1][:, 8:16], lhsT=tri[:, 128:256], rhs=gpad[:, 0:8], start=True, stop=True),
             reads=[t_gpad, t_const], writes=[c.t_ps[1]])
        for cc in range(2):
            P.op("pe", lambda e, cc=cc: e.matmul(c.ps[1][:, 16 + cc * 8:24 + cc * 8], lhsT=csel[:, cc * 128:(cc + 1) * 128], rhs=gpad[:, 0:8],
                                                start=True, stop=True), reads=[t_gpad, t_const], writes=[c.t_ps[1]])
        P.op("pe", lambda e: e.matmul(c.ps[1][:, 128:256], lhsT=gpad[:, :], rhs=tri[:, 0:128], start=True, stop=True),
             reads=[t_gpad, t_const], writes=[c.t_ps[1]])
        P.op("dve", lambda e, t=t: e.tensor_copy(out=gc[:, 0, :], in_=c.ps[1][:, 0:8]), reads=[c.t_ps[1]], writes=[t_sc])
        P.op("act", lambda e, t=t: e.activation(out=gam[:, 0, :], in_=c.ps[1][:, 0:8], func=AF.Exp), reads=[c.t_ps[1]], writes=[t_sc])
        P.op("dve", lambda e, t=t: e.tensor_tensor(out=bgam[:, 0, :], in0=gam[:, 0, :], in1=beta[:, 0, :], op=ALU.mult),
             reads=[t_sc, t_beta], writes=[t_sc])
        P.op("act", lambda e, t=t: e.activation(out=etl[:, 0, 0, :], in_=c.ps[1][:, 8:16], func=AF.Exp), reads=[c.t_ps[1]], writes=[t_sc])
        P.op("dve", lambda e, t=t: e.tensor_scalar(out=etl[:, 0, 1, :], in0=etl[:, 0, 0, :], scalar1=rowmask[:, 1:2], scalar2=None, op0=ALU.mult),
             reads=[t_sc, t_const], writes=[t_sc])
        P.op("dve", lambda e, t=t: e.tensor_scalar(out=etl[:, 0, 0, :], in0=etl[:, 0, 0, :], scalar1=rowmask[:, 0:1], scalar2=None, op0=ALU.mult),
             reads=[t_sc, t_const], writes=[t_sc])
        P.op("act", lambda e, t=t: e.activation(out=egl[:, 0, :, :], in_=c.ps[1][:, 16:32].rearrange("p (a n) -> p a n", a=2), func=AF.Exp),
             reads=[c.t_ps[1]], writes=[t_sc])
        P.op("dve", lambda e, t=t: e.tensor_copy(out=gcT[:, 0, :], in_=c.ps[1][:, 128:256]), reads=[c.t_ps[1]], writes=[t_sc])

    wt = [A("wt%d" % i, [128, 8, 128], BF16) for i in range(2)]
    t_wt = toks(2)
    t_xbuf = toks(16)
    P.op("pool", lambda e: e.memset(xflat[:, :], 0.0), writes=t_xbuf)
    qhT = A("qhT", [128, 4, 512], BF16)
    khT = A("khT", [128, 4, 512], BF16)
    vT = A("vT", [128, 8, 512], BF16)
    zsT = A("zsT", [128, 8, 512], BF16)
    mixblk = A("mixblk", [128, 8, 128], BF16)
    t_qhT, t_khT, t_vT, t_zsT, t_mixblk = toks(5)
    yT = scr[:, 5 * 1024:5 * 1024 + 512]
    sqb = A("sqb", [128, 512], BF16)
    rn = scr[:, 5 * 1024 + 512:6 * 1024]
    t_yT, t_sqb, t_rn = toks(3)
    f32big = lambda n: A(n, [128, 8, 128], F32)
    bfbig = lambda n: A(n, [128, 8, 128], BF16)
    vb, kbg, attn, attnT, Lm, Um, L2, U2, Pm, P2, nwT, qdT = [bfbig("g_%d" % i) for i in range(12)]
    t_vb, t_kbg, t_attn, t_attnT, t_L, t_U, t_L2, t_U2, t_Pm, t_P2, t_nwT, t_qdT = toks(12)
    ktail2 = A("ktail2", [128, 2, 8, 128], BF16)
    t_ktail2 = Tok()
    scr4 = lambda i: scr[:, i * 1024:(i + 1) * 1024].rearrange("p (a n) -> p a n", a=8)
    Dl, tmpf, grow = scr4(0), scr4(1), scr4(2)
    t_Dl, t_tmpf, t_grow = toks(3)
    BDm = scr4(3)
    t_BD = Tok()
    vnew = A("g_vnew", [128, 8, 128], BF16)
    t_vn = toks(2)
    t_zh, t_wh = toks(2), toks(2)
    Sst = A("g_S", [128, 8, 128], F32)
    Sbf = A("g_Sbf", [128, 8, 128], BF16)
    t_S, t_Sbf = toks(2), toks(2)
    P.op("pool", lambda e: e.memset(Sst[:], 0.0), writes=t_S)
    P.op("pool", lambda e: e.memset(Sbf[:], 0.0), writes=t_Sbf)
    P.op("pool", lambda e: e.memset(vnew[:], 0.0), writes=t_vn)
    sq8 = bfbig("g_sq8")
    rs8 = scr4(4)
    t_sq8, t_rs8 = toks(2)
    ps2 = lambda k: c.psall[:, k * 1024:(k + 1) * 1024]
    t_ps2 = [c.t_pspair[k] for k in range(4)]
    X, Y, Z, W = 0, 1, 2, 3
    Xall = [t_ps2[X]]
    Zall = [t_ps2[Z]] + t_zh
    Wall = [t_ps2[W]] + t_wh

    def v3(ap):
        return ap.rearrange("p (a n) -> p a n", a=8)

    wi = 0
    for tb in range(nblocks):
        bcols = slice(tb * 512, (tb + 1) * 512)
        for ct in range(24):
            wb = wi % 2
            wi += 1
            load_w(c, win, ct * 128, 128, wt[wb], t_wt[wb])
            bank = 4 + (ct % 2)
            proj_fm(c, wt[wb], t_wt[wb], hnT, t_hnT, tb, bank, wtoks=Zall)
            if ct >= 16:
                P.op("act", lambda e, ct=ct, bank=bank: e.activation(out=zsT[:, ct - 16, :], in_=c.ps[bank][:, :], func=AF.Silu),
                     reads=[*Zall, *Zall], writes=[t_zsT])
                continue
            P.op("dve", lambda e, ct=ct: e.tensor_copy(out=xbuf[:, ct, 0:3], in_=xbuf[:, ct, 512:515]), reads=[t_xbuf[ct]], writes=[t_xbuf[ct]])
            P.op("act", lambda e, ct=ct, bank=bank: e.activation(out=xbuf[:, ct, 3:515], in_=c.ps[bank][:, :], func=AF.Copy),
                 reads=[*Zall, *Zall], writes=[t_xbuf[ct]])
            cb = 6 + (ct % 2)
            for tap in range(4):
                P.op("pe", lambda e, ct=ct, tap=tap, cb=cb: e.matmul(c.ps[cb][:, :], lhsT=dg[:, ct * 4 + tap, :], rhs=xbuf[:, ct, tap:tap + 512],
                                                                    start=(tap == 0), stop=(tap == 3)),
                     reads=[t_xbuf[ct], t_const], writes=[*Wall])
            if ct >= 8:
                P.op("act", lambda e, ct=ct, cb=cb: e.activation(out=vT[:, ct - 8, :], in_=c.ps[cb][:, :], func=AF.Silu),
                     reads=[*Wall, *Wall], writes=[t_vT])
                continue
            P.op("act", lambda e, cb=cb: e.activation(out=yT[:], in_=c.ps[cb][:, :], func=AF.Silu), reads=[*Wall, *Wall], writes=[t_yT])
            P.op("act", lambda e: e.activation(out=sqb[:], in_=yT[:], func=AF.Square), reads=[t_yT], writes=[t_sqb])
            P.op("pe", lambda e, cb=cb: e.matmul(c.ps[cb][:, :], lhsT=onesb[:, :], rhs=sqb[:], start=True, stop=True),
                 reads=[t_sqb, t_const], writes=[*Wall])
            P.op("act", lambda e, cb=cb: e.activation(out=rn[:], in_=c.ps[cb][:, :], func=AF.Ln, bias=EPS), reads=[*Wall], writes=[t_rn])
            P.op("act", lambda e: e.activation(out=rn[:], in_=rn[:], func=AF.Exp, scale=-0.5), reads=[t_rn], writes=[t_rn])
            if ct < 4:
                P.op("dve", lambda e, ct=ct: e.scalar_tensor_tensor(out=qhT[:, ct, :], in0=yT[:], scalar=float(128 ** -0.5), in1=rn[:],
                                                                   op0=ALU.mult, op1=ALU.mult), reads=[t_yT, t_rn], writes=[t_qhT])
            else:
                P.op("dve", lambda e, ct=ct: e.tensor_tensor(out=khT[:, ct - 4, :], in0=yT[:], in1=rn[:], op=ALU.mult),
                     reads=[t_yT, t_rn], writes=[t_khT])
        for tt in range(4):
            t = tb * 4 + tt
            tk = slice(tt * 128, (tt + 1) * 128)
            scalars(t)
            zb = c.psall.bitcast(BF16)[:, Z * 2048:(Z + 1) * 2048]
            for hk in range(4):
                P.op("pe", lambda e, hk=hk, tk=tk, zb=zb: e.transpose(out=zb[:, hk * 128:(hk + 1) * 128], in_=khT[:, hk, tk], identity=c.ident[:]),
                     reads=[t_khT, c.t_ident], writes=[*Zall, *Zall, *Zall])
            for hv in range(8):
                P.op("pe", lambda e, hv=hv, tk=tk, zb=zb: e.transpose(out=zb[:, 512 + hv * 128:512 + (hv + 1) * 128], in_=vT[:, hv, tk], identity=c.ident[:]),
                     reads=[t_vT, c.t_ident], writes=[*Zall, *Zall, *Zall])
            kt4 = zb[:, 0:512].rearrange("p (a n) -> p a n", a=4).unsqueeze(2).broadcast_to([128, 4, 2, 128])
            P.op("dve", lambda e, zb=zb, t=t: e.tensor_tensor(out=vb[:, :, :], in0=zb[:, 512:1536].rearrange("p (a n) -> p a n", a=8),
                                                            in1=beta[:, 0, :].unsqueeze(2).broadcast_to([128, 8, 128]), op=ALU.mult),
                 reads=[*Zall, t_beta], writes=[t_vb])
            P.op("dve", lambda e, kt4=kt4, t=t: e.tensor_tensor(out=kbg[:, :, :].rearrange("p (a b) n -> p a b n", b=2), in0=kt4,
                                                              in1=bgam[:, 0, :].rearrange("p (a b) -> p a b", b=2).unsqueeze(3).broadcast_to([128, 4, 2, 128]),
                                                              op=ALU.mult), reads=[*Zall, t_sc], writes=[t_kbg])
            for cc in range(2):
                P.op("dve", lambda e, kt4=kt4, t=t, cc=cc: e.tensor_tensor(
                    out=ktail2[:, cc, :, :].rearrange("p (a b) n -> p a b n", b=2), in0=kt4,
                    in1=etl[:, 0, cc, :].rearrange("p (a b) -> p a b", b=2).unsqueeze(3).broadcast_to([128, 4, 2, 128]), op=ALU.mult),
                    reads=[*Zall, t_sc], writes=[t_ktail2])
            for hk in range(4):
                P.op("pe", lambda e, hk=hk, tk=tk: e.matmul(c.ps[0][:, hk * 128:(hk + 1) * 128], lhsT=khT[:, hk, tk], rhs=khT[:, hk, tk],
                                                           start=True, stop=True), reads=[t_khT], writes=[c.t_ps[0], t_ps2[X]])
            for hk in range(4):
                P.op("pe", lambda e, hk=hk, tk=tk: e.matmul(c.ps[1][:, hk * 128:(hk + 1) * 128], lhsT=qhT[:, hk, tk], rhs=khT[:, hk, tk],
                                                           start=True, stop=True), reads=[t_qhT, t_khT], writes=[c.t_ps[1], t_ps2[X]])
            P.op("pool", lambda e, t=t: e.tensor_tensor(out=BDm[:, :, :], in0=gcT[:, 0, :].unsqueeze(1).broadcast_to([128, 8, 128]),
                                                       in1=delta[:, :].unsqueeze(2).broadcast_to([128, 8, 128]), op=ALU.mult),
                 reads=[t_sc, t_const], writes=[t_BD])
            for hf in range(2):
                P.op("pe", lambda e, hf=hf: e.matmul(c.ps[2 + hf][:, :], lhsT=onesf[:, :], rhs=BDm[:, hf * 4:(hf + 1) * 4, :], start=True, stop=True),
                     reads=[t_BD, t_const], writes=[c.t_ps[2 + hf], t_ps2[Y]])
            P.op("dve", lambda e, t=t: e.tensor_tensor(out=tmpf[:, :, :], in0=gc[:, 0, :].unsqueeze(2).broadcast_to([128, 8, 128]),
                                                      in1=v3(ps2(Y)), op=ALU.subtract), reads=[t_ps2[Y], t_sc], writes=[t_tmpf])
            P.op("pool", lambda e: e.tensor_scalar(out=tmpf[:, :, :], in0=tmpf[:, :, :], scalar1=0.0, scalar2=None, op0=ALU.min),
                 reads=[t_tmpf], writes=[t_tmpf])
            P.op("act", lambda e: e.activation(out=Dl[:, :, :], in_=tmpf[:, :, :], func=AF.Exp), reads=[t_tmpf], writes=[t_Dl])
            P.op("act", lambda e: e.activation(out=grow[:, :, :], in_=v3(ps2(Y)), func=AF.Exp), reads=[t_ps2[Y]], writes=[t_grow])
            P.op("dve", lambda e, tk=tk: e.tensor_tensor(out=qdT[:, :, :].rearrange("p (a b) n -> p a b n", b=2),
                                                        in0=qhT[:, :, tk].unsqueeze(2).broadcast_to([128, 4, 2, 128]),
                                                        in1=grow[:, :, :].rearrange("p (a b) n -> p a b n", b=2), op=ALU.mult),
                 reads=[t_qhT, t_grow], writes=[t_qdT])
            kk4 = c.ps[0][:, :].rearrange("p (a n) -> p a n", a=4).unsqueeze(2).broadcast_to([128, 4, 2, 128])
            qk4 = c.ps[1][:, :].rearrange("p (a n) -> p a n", a=4).unsqueeze(2).broadcast_to([128, 4, 2, 128])
            D4 = Dl[:, :, :].rearrange("p (a b) n -> p a b n", b=2)
            P.op("dve", lambda e, kk4=kk4, D4=D4: e.tensor_tensor(out=tmpf[:, :, :].rearrange("p (a b) n -> p a b n", b=2), in0=kk4, in1=D4, op=ALU.mult),
                 reads=[t_ps2[X], t_Dl], writes=[t_tmpf])
            P.op("pool", lambda e, t=t: e.tensor_tensor(out=tmpf[:, :, :], in0=tmpf[:, :, :], in1=beta[:, 0, :].unsqueeze(2).broadcast_to([128, 8, 128]),
                                                       op=ALU.mult), reads=[t_tmpf, t_beta], writes=[t_tmpf])
            P.op("pool", lambda e: e.tensor_tensor(out=Lm[:, :, :], in0=tmpf[:, :, :], in1=mstrict[:, :].unsqueeze(1).broadcast_to([128, 8, 128]),
                                                  op=ALU.mult), reads=[t_tmpf, t_const], writes=[t_L])
            P.op("dve", lambda e, qk4=qk4, D4=D4: e.tensor_tensor(out=grow[:, :, :].rearrange("p (a b) n -> p a b n", b=2), in0=qk4, in1=D4, op=ALU.mult),
                 reads=[t_ps2[X], t_Dl, t_qdT], writes=[t_grow])
            P.op("pool", lambda e: e.tensor_tensor(out=attn[:, :, :], in0=grow[:, :, :], in1=mcausal[:, :].unsqueeze(1).broadcast_to([128, 8, 128]),
                                                  op=ALU.mult), reads=[t_grow, t_const], writes=[t_attn])
            wbv = c.psall.bitcast(BF16)[:, W * 2048:(W + 1) * 2048]
            for hv in range(8):
                P.op("pe", lambda e, hv=hv, zb=zb: e.transpose(out=zb[:, hv * 128:(hv + 1) * 128], in_=Lm[:, hv, :], identity=c.ident[:]),
                     reads=[t_L, c.t_ident], writes=[*Zall, *Zall, *Zall])
            for hv in range(8):
                P.op("pe", lambda e, hv=hv, wbv=wbv: e.transpose(out=wbv[:, hv * 128:(hv + 1) * 128], in_=attn[:, hv, :], identity=c.ident[:]),
                     reads=[t_attn, c.t_ident], writes=[*Wall, *Wall, *Wall])
            P.op("act", lambda e, zb=zb: e.activation(out=Um[:, :, :], in_=zb[:, 0:1024].rearrange("p (a n) -> p a n", a=8), func=AF.Copy),
                 reads=[*Zall], writes=[t_U])
            P.op("act", lambda e, wbv=wbv: e.activation(out=attnT[:, :, :], in_=wbv[:, 0:1024].rearrange("p (a n) -> p a n", a=8), func=AF.Copy),
                 reads=[*Wall], writes=[t_attnT])
            P.op("dve", lambda e: e.tensor_tensor(out=Pm[:, :, :], in0=c.ident[:, :].unsqueeze(1).broadcast_to([128, 8, 128]), in1=Um[:, :, :],
                                                 op=ALU.subtract), reads=[t_U, c.t_ident], writes=[t_Pm])
            Lc, Uc, Ln_, Un_, Pc, Pn = Lm, Um, L2, U2, Pm, P2
            tLc, tUc, tLn, tUn, tPc, tPn = t_L, t_U, t_L2, t_U2, t_Pm, t_P2
            for lev in range(5):
                for hv in range(8):
                    P.op("pe", lambda e, hv=hv, Lc=Lc, Uc=Uc: e.matmul(c.ps[(hv // 4)][:, (hv % 4) * 128:(hv % 4 + 1) * 128], lhsT=Uc[:, hv, :], rhs=Lc[:, hv, :],
                                                                      start=True, stop=True),
                         reads=[tLc, tUc], writes=[t_ps2[X], c.t_ps[hv // 4]])
                if lev < 4:
                    for hv in range(8):
                        P.op("pe", lambda e, hv=hv, Lc=Lc, Uc=Uc: e.matmul(c.ps[2 + (hv // 4)][:, (hv % 4) * 128:(hv % 4 + 1) * 128], lhsT=Lc[:, hv, :],
                                                                          rhs=Uc[:, hv, :], start=True, stop=True),
                             reads=[tLc, tUc], writes=[t_ps2[Y], c.t_ps[2 + hv // 4]])
                P.op("act", lambda e, Ln_=Ln_: e.activation(out=Ln_[:, :, :], in_=v3(ps2(X)), func=AF.Copy), reads=[t_ps2[X]], writes=[tLn])
                if lev < 4:
                    P.op("dve", lambda e, Un_=Un_: e.tensor_copy(out=Un_[:, :, :], in_=v3(ps2(Y))), reads=[t_ps2[Y]], writes=[tUn])
                for hv in range(8):
                    P.op("pe", lambda e, hv=hv, Ln_=Ln_, Pc=Pc: e.matmul(c.ps[4 + (hv // 4)][:, (hv % 4) * 128:(hv % 4 + 1) * 128], lhsT=Ln_[:, hv, :],
                                                                        rhs=Pc[:, hv, :], start=True, stop=True),
                         reads=[tLn, tPc], writes=[*Zall, *Zall])
                P.op("dve", lambda e, Pc=Pc, Pn=Pn: e.tensor_tensor(out=Pn[:, :, :], in0=v3(ps2(Z)), in1=Pc[:, :, :], op=ALU.add),
                     reads=[*Zall, tPc], writes=[tPn])
                Lc, Ln_, tLc, tLn = Ln_, Lc, tLn, tLc
                Uc, Un_, tUc, tUn = Un_, Uc, tUn, tUc
                Pc, Pn, tPc, tPn = Pn, Pc, tPn, tPc
            TT, tTT = Pc, tPc
            for hv in range(8):
                P.op("pe", lambda e, hv=hv, TT=TT: e.matmul(c.ps[2 + (hv // 4)][:, (hv % 4) * 128:(hv % 4 + 1) * 128], lhsT=kbg[:, hv, :], rhs=TT[:, hv, :],
                                                           start=True, stop=True), reads=[t_kbg, tTT], writes=[t_ps2[Y], c.t_ps[2 + hv // 4]])
            P.op("act", lambda e: e.activation(out=nwT[:, :, :], in_=v3(ps2(Y)), func=AF.Copy, scale=-1.0), reads=[t_ps2[Y]], writes=[t_nwT])
            for cc in range(2):
                rr = slice(cc * 64, (cc + 1) * 64)
                for g in range(2):
                    g4 = slice(4 * g, 4 * g + 4)
                    for hv in range(4 * g, 4 * g + 4):
                        zs = c.ps[4 + g][:, (hv % 4) * 128:(hv % 4 + 1) * 128]
                        P.op("pe", lambda e, hv=hv, TT=TT, zs=zs: e.matmul(zs, lhsT=TT[:, hv, :], rhs=vb[:, hv, :], start=True, stop=False),
                             reads=[tTT, t_vb], writes=[t_zh[g]])
                        P.op("pe", lambda e, hv=hv, zs=zs: e.matmul(zs, lhsT=nwT[:, hv, :], rhs=Sbf[:, hv, :], start=False, stop=True),
                             reads=[t_nwT, t_Sbf[g]], writes=[t_zh[g]])
                    zin = c.ps[4 + g][rr, :].rearrange("p (a n) -> p a n", a=4)
                    if g == 0:
                        P.op("act", lambda e, rr=rr, g4=g4, zin=zin: e.activation(out=vnew[rr, g4, :], in_=zin, func=AF.Copy),
                             reads=[t_zh[g]], writes=[t_vn[g]])
                    else:
                        P.op("dve", lambda e, rr=rr, g4=g4, zin=zin: e.tensor_copy(out=vnew[rr, g4, :], in_=zin),
                             reads=[t_zh[g]], writes=[t_vn[g]])
                for g in range(2):
                    g4 = slice(4 * g, 4 * g + 4)
                    for hv in range(4 * g, 4 * g + 4):
                        ob = c.ps[hv // 4]
                        o0 = (hv % 4) * 128
                        oc = slice(o0 + cc * 64, o0 + (cc + 1) * 64)
                        ws = c.ps[6 + g][:, (hv % 4) * 128:(hv % 4 + 1) * 128]
                        P.op("pe", lambda e, hv=hv, ob=ob, oc=oc, cc=cc: e.matmul(ob[:, oc], lhsT=Sbf[:, hv, :], rhs=qdT[:, hv, cc * 64:(cc + 1) * 64],
                                                                                 start=True, stop=False),
                             reads=[t_Sbf[g], t_qdT], writes=[*Xall])
                        P.op("pe", lambda e, hv=hv, ob=ob, oc=oc, cc=cc: e.matmul(ob[:, oc], lhsT=vnew[:, hv, :], rhs=attnT[:, hv, cc * 64:(cc + 1) * 64],
                                                                                 start=False, stop=True),
                             reads=[t_vn[g], t_attnT], writes=[*Xall])
                        P.op("pe", lambda e, hv=hv, cc=cc, ws=ws: e.matmul(ws, lhsT=ktail2[:, cc, hv, :], rhs=vnew[:, hv, :], start=True, stop=True),
                             reads=[t_ktail2, t_vn[g]], writes=[t_wh[g]])
                    win4 = c.ps[6 + g][:, :].rearrange("p (a n) -> p a n", a=4)
                    P.op("dve", lambda e, g4=g4, cc=cc: e.tensor_tensor(out=Sst[:, g4, :], in0=Sst[:, g4, :],
                                                                        in1=egl[:, 0, cc, g4].unsqueeze(2).broadcast_to([128, 4, 128]), op=ALU.mult),
                         reads=[t_S[g], t_sc], writes=[t_S[g]])
                    P.op("dve", lambda e, g4=g4, win4=win4: e.tensor_tensor(out=Sst[:, g4, :], in0=Sst[:, g4, :], in1=win4, op=ALU.add),
                         reads=[t_S[g], t_wh[g]], writes=[t_S[g]])
                    if g == 0:
                        P.op("act", lambda e, g4=g4: e.activation(out=Sbf[:, g4, :], in_=Sst[:, g4, :], func=AF.Copy), reads=[t_S[g]], writes=[t_Sbf[g]])
                    else:
                        P.op("pool", lambda e, g4=g4: e.tensor_copy(out=Sbf[:, g4, :], in_=Sst[:, g4, :]), reads=[t_S[g]], writes=[t_Sbf[g]])
            P.op("act", lambda e: e.activation(out=sq8[:, :, :], in_=v3(ps2(X)), func=AF.Square), reads=[t_ps2[X]], writes=[t_sq8])
            for hf in range(2):
                P.op("pe", lambda e, hf=hf: e.matmul(c.ps[2 + hf][:, :], lhsT=onesb[:, :], rhs=sq8[:, hf * 4:(hf + 1) * 4, :], start=True, stop=True),
                     reads=[t_sq8, t_const], writes=[t_ps2[Y], c.t_ps[2 + hf]])
            P.op("act", lambda e: e.activation(out=rs8[:, :, :], in_=v3(ps2(Y)), func=AF.Ln, scale=1.0 / 128, bias=EPS), reads=[t_ps2[Y]], writes=[t_rs8])
            P.op("act", lambda e: e.activation(out=rs8[:, :, :], in_=rs8[:, :, :], func=AF.Exp, scale=-0.5), reads=[t_rs8], writes=[t_rs8])
            P.op("dve", lambda e: e.tensor_tensor(out=rs8[:, :, :], in0=v3(ps2(X)), in1=rs8[:, :, :], op=ALU.mult), reads=[t_ps2[X], t_rs8], writes=[t_rs8])
            P.op("dve", lambda e, tk=tk: e.scalar_tensor_tensor(out=mixblk[:, :, :], in0=rs8[:, :, :], scalar=onw[:, 0:1], in1=zsT[:, :, tk],
                                                               op0=ALU.mult, op1=ALU.mult), reads=[t_rs8, t_zsT, t_const], writes=[t_mixblk])
            P.dma("sp", lambda e, t=t: e.dma_start(out=mixT[:, :, t * 128:(t + 1) * 128].rearrange("c p n -> p c n"), in_=mixblk[:, :, :]),
                  reads=[t_mixblk])

    if nblocks < 8:
        for t in range(nblocks * 4, NT):
            P.dma("sp", lambda e, t=t: e.dma_start(out=mixT[:, :, t * 128:(t + 1) * 128].rearrange("c p n -> p c n"), in_=mixblk[:, :, :]))
    P.barrier()
    outproj(c, mixT, None, wout, pout, hnT[:, :, :].rearrange("p c n -> p (c n)"), t_pout=(fused or {}).get("t_pout"))
    if fused is None:
        P.emit()
    return nc


def odd_consts():
    bf = ml_dtypes.bfloat16
    a = np.arange(128)[:, None]
    b = np.arange(128)[None, :]
    same = (a // 64) == (b // 64)
    U = (same & (a <= b)).astype(np.float32)
    Aft = (same & (a > b)).astype(np.float32)
    csel = np.zeros((128, 256), np.float32)
    csel[0:64, 0:128] = 1.0
    csel[64:128, 128:256] = 1.0
    mstrict = (same & (a > b)).astype(np.float32)
    mcausal = (same & (a >= b)).astype(np.float32)
    delta = np.zeros((128, 8), np.float32)
    delta[np.arange(8), np.arange(8)] = 1.0
    rowmask = np.zeros((128, 2), np.float32)
    rowmask[0:64, 0] = 1.0
    rowmask[64:128, 1] = 1.0
    return {"tri": np.concatenate([U, Aft, np.zeros((128, 128), np.float32)], axis=1), "csel": csel, "mstrict": mstrict.astype(bf),
            "mcausal": mcausal.astype(bf), "delta": delta, "rowmask": rowmask, "ident": np.eye(128, dtype=np.float32).astype(bf)}


def odd_inputs(hprev, p0, p1, norm_w_l, w_in, conv_w, dt_bias, a_log, onw, w_out, j):
    q = w_in[:, 0 + j * 512: 0 + (j + 1) * 512]
    k = w_in[:, 1024 + j * 512: 1024 + (j + 1) * 512]
    v = w_in[:, 2048 + j * 1024: 2048 + (j + 1) * 1024]
    z = w_in[:, 4096 + j * 1024: 4096 + (j + 1) * 1024]
    be = w_in[:, 6144 + j * 8: 6144 + (j + 1) * 8]
    al = w_in[:, 6160 + j * 8: 6160 + (j + 1) * 8]
    pad = np.zeros((D, 112), np.float32)
    win = np.ascontiguousarray(np.concatenate([q, k, v, z, be, al, pad], axis=1))
    cwc = np.concatenate([conv_w[:, 0 + j * 512: (j + 1) * 512], conv_w[:, 1024 + j * 512: 1024 + (j + 1) * 512],
                          conv_w[:, 2048 + j * 1024: 2048 + (j + 1) * 1024]], axis=1)
    cw = np.ascontiguousarray(cwc.reshape(4, 16, 128).transpose(2, 1, 0).reshape(128, 64))
    hc = np.concatenate([dt_bias[j * 8:(j + 1) * 8], a_log[j * 8:(j + 1) * 8]]).reshape(1, 16).astype(np.float32)
    m = {"hprev": hprev, "p0": p0, "p1": p1, "nw": norm_w_l.reshape(1, D), "win": win,
         "wout": np.ascontiguousarray(w_out[j * 1024:(j + 1) * 1024]), "cw": cw, "hconst": hc,
         "onw": np.ascontiguousarray(onw.reshape(128, 1))}
    m.update(odd_consts())
    return m


def build_final():
    nc = bass.Bass("TRN2", target_bir_lowering=False)
    P = Prog(nc)
    H = S // 2
    dram = lambda n, shp, dt=F32: nc.dram_tensor(n, shp, dt, kind="ExternalInput").ap()
    hprev = dram("hprev", [H, D])
    p0 = dram("p0", [H, D])
    p1 = dram("p1", [H, D])
    nw = dram("nw", [1, D])
    out = nc.dram_tensor("out", [H, D], F32, kind="ExternalOutput").ap()
    c = common_setup(nc, P, None)
    phase1(c, hprev, p0, p1, nw, out, None, None, ntiles=H // 128)
    P.emit()
    return nc


def build_fused(pairs=((0, 1), (2, 3), (4, 5), (6, 7)), use_cc=True, nlayers=4):
    pairs = [list(p) for p in pairs]
    nc = bass.Bass("TRN2", target_bir_lowering=False)
    P = Prog(nc)
    x = nc.dram_tensor("x", [S, D], F32, kind="ExternalInput").ap()
    fnw = nc.dram_tensor("fnw", [1, D], F32, kind="ExternalInput").ap()
    out = nc.dram_tensor("out", [S, D], F32, kind="ExternalOutput").ap()
    h_d = nc.dram_tensor("h_d", [S, D], F32).ap()
    pout_d = nc.dram_tensor("pout_d", [S, D], F32).ap()
    psum_d = nc.dram_tensor("psum_d", [S, D], F32).ap()
    mixT = nc.dram_tensor("mixT", [8, 128, S], BF16).ap()
    c = common_setup(nc, P, None)
    for layer in range(nlayers):
        fz = {"nc": nc, "P": P, "c": c, "pre": "L%d_" % layer, "hprev": (x if layer == 0 else h_d), "psum": (None if layer == 0 else psum_d),
              "hout": h_d, "pout": pout_d, "mixT": mixT, "t_pout": toks(4)}
        if layer % 2 == 0:
            build_even(fused=fz)
        else:
            build_odd(fused=fz)
        if use_cc:
            for k, r0 in enumerate(range(0, S, 1024)):
                P.cc(lambda e, r0=r0: e.collective_compute("AllReduce", ALU.add, replica_groups=pairs,
                                                           ins=[pout_d[r0:r0 + 1024, :].opt()], outs=[psum_d[r0:r0 + 1024, :].opt()]),
                     writes=[fz["t_pout"][k]])
        else:
            for r0 in range(0, S, 256):
                P.dma("sp", lambda e, r0=r0: e.dma_start(out=psum_d[r0:r0 + 256, :], in_=pout_d[r0:r0 + 256, :]))
        P.barrier()
    c.arena.reset()
    c.t_ps = c.t_ps_single
    phase1(c, h_d, psum_d, None, fnw, out, None, None)
    P.emit()
    return nc


_CACHE = {}


def _prog(name):
    if name not in _CACHE:
        _CACHE[name] = {"even": build_even, "odd": build_odd, "final": build_final, "fused": build_fused}[name]()
    return _CACHE[name]


def fused_inputs(b, j, x, norm_w, final_norm_w, even_w_in, even_w_out, hgrn_lb_logits, hgrn_norm_w,
                 odd_w_in, odd_conv_w, odd_dt_bias, odd_a_log, odd_norm_w, odd_w_out):
    m = {"x": x[b], "fnw": final_norm_w.reshape(1, D)}
    for layer in range(4):
        jl = layer // 2
        if layer % 2 == 0:
            lm = even_inputs(None, None, None, norm_w[layer], even_w_in[jl], even_w_out[jl], hgrn_lb_logits, hgrn_norm_w[jl], jl, j)
        else:
            lm = odd_inputs(None, None, None, norm_w[layer], odd_w_in[jl], odd_conv_w[jl], odd_dt_bias[jl], odd_a_log[jl],
                            odd_norm_w[jl], odd_w_out[jl], j)
        for k, v in lm.items():
            if k in ("hprev", "p0", "p1"):
                continue
            if k == "ident":
                m["ident"] = v
            else:
                m["L%d_%s" % (layer, k)] = v
    return m


def kernel(x, norm_w, final_norm_w, even_w_in, even_w_out, hgrn_lb_logits, hgrn_norm_w,
           odd_w_in, odd_conv_w, odd_dt_bias, odd_a_log, odd_norm_w, odd_w_out):
    f = lambda a: np.ascontiguousarray(np.asarray(a, dtype=np.float32))
    args = [f(a) for a in (x, norm_w, final_norm_w, even_w_in, even_w_out, hgrn_lb_logits, hgrn_norm_w,
                           odd_w_in, odd_conv_w, odd_dt_bias, odd_a_log, odd_norm_w, odd_w_out)]
    B = args[0].shape[0]
    cores = [(b, j) for b in range(B) for j in range(2)]
    maps = [fused_inputs(b, j, *args) for (b, j) in cores]
    res = run_bass_kernel_spmd(_prog("fused"), maps, core_ids=list(range(8))).results
    out = np.stack([np.asarray(res[2 * b]["out"]) for b in range(B)])
    return out.astype(np.float32)
```

```python
import contextlib
import numpy as np
import ml_dtypes
import concourse.bass as bass
import concourse.mybir as mybir
from concourse.bass_utils import run_bass_kernel_spmd

F32 = mybir.dt.float32
BF16 = mybir.dt.bfloat16
AF = mybir.ActivationFunctionType
ALU = mybir.AluOpType

S = 4096
D = 1024
NT = S // 128
EPS = 1e-6


def sl(s, n, st=1):
    return slice(s, s + (n - 1) * st + 1, st)


class Tok:
    __slots__ = ("lw", "rd")

    def __init__(self):
        self.lw = None
        self.rd = {}


def toks(n):
    return [Tok() for _ in range(n)]


class Op:
    __slots__ = ("issuer", "chan", "fn", "deps", "sig", "val", "is_dma", "inc")

    def __init__(self, issuer, chan, fn, is_dma, inc=None):
        self.inc = inc if inc is not None else (16 if is_dma else 1)
        self.issuer = issuer
        self.chan = chan
        self.fn = fn
        self.deps = set()
        self.sig = False
        self.val = 0
        self.is_dma = is_dma


class Prog:
    def __init__(self, nc, n_dma_ch=12):
        self.nc = nc
        self.ops = []
        self.last_on_chan = {}
        self.dma_rr = 0
        self.n_dma_ch = n_dma_ch
        self.bar_deps = set()

    def _add(self, issuer, chan, fn, reads, writes, is_dma, inc=None):
        op = Op(issuer, chan, fn, is_dma, inc)
        idx = len(self.ops)
        op.deps |= self.bar_deps
        for t in reads:
            if t.lw is not None:
                op.deps.add(t.lw)
        for t in writes:
            if t.lw is not None:
                w = self.ops[t.lw]
                if is_dma or w.chan != chan:
                    op.deps.add(t.lw)
            for c, r in t.rd.items():
                if is_dma or c != chan:
                    op.deps.add(r)
        if is_dma and chan in self.last_on_chan:
            op.deps.add(self.last_on_chan[chan])
        self.last_on_chan[chan] = idx
        for t in reads:
            t.rd[chan] = idx
        for t in writes:
            t.lw = idx
            t.rd = {}
        self.ops.append(op)
        return idx

    def op(self, eng, fn, reads=(), writes=()):
        return self._add(eng, eng, fn, reads, writes, False)

    def dma(self, issuer, fn, reads=(), writes=()):
        chan = "dma%d" % self.dma_rr
        self.dma_rr = (self.dma_rr + 1) % self.n_dma_ch
        return self._add(issuer, chan, fn, reads, writes, True)

    def cc(self, fn, reads=(), writes=()):
        return self._add("pool", "cc", fn, reads, writes, True, inc=1)

    def barrier(self):
        self.bar_deps = set(self.last_on_chan.values())

    def emit(self):
        nc = self.nc
        ops = self.ops
        for op in ops:
            for d in op.deps:
                ops[d].sig = True
        cnt = {}
        for op in ops:
            if op.is_dma:
                cnt[op.chan] = cnt.get(op.chan, 0) + op.inc
                op.val = cnt[op.chan]
                op.sig = True
            elif op.sig:
                cnt[op.chan] = cnt.get(op.chan, 0) + 1
                op.val = cnt[op.chan]
        chans = sorted(set(op.chan for op in ops))
        issuers = sorted(set(op.issuer for op in ops) | {"sp"})
        with contextlib.ExitStack() as es:
            sems = {c: es.enter_context(nc.semaphore("s_" + c)) for c in chans}
            block = es.enter_context(nc.Block())
            engmap = {"pe": block.tensor, "act": block.scalar, "dve": block.vector,
                      "pool": block.gpsimd, "sp": block.sync}
            final = {c: cnt.get(c, 0) for c in chans}

            def make(issuer):
                def body(e):
                    known = {}
                    for op in ops:
                        if op.issuer != issuer:
                            continue
                        need = {}
                        for d in op.deps:
                            dop = ops[d]
                            if dop.val > need.get(dop.chan, 0):
                                need[dop.chan] = dop.val
                        for c, v in need.items():
                            if known.get(c, 0) < v:
                                e.wait_ge(sems[c], v)
                                known[c] = v
                        ins = op.fn(e)
                        if op.sig:
                            ins.then_inc(sems[op.chan], op.inc)
                    if issuer == "sp":
                        for c, v in final.items():
                            if v > 0:
                                e.wait_ge(sems[c], v)
                return body

            for issuer in issuers:
                engmap[issuer](make(issuer))
        return nc


class Ctx:
    pass


class Arena:
    def __init__(self, nc, nbytes=212480):
        self.n = nbytes // 2
        self.t = nc.alloc_sbuf_tensor("arena", [128, self.n], BF16)
        self.cur = 0
        self.base = 0

    def alloc(self, name, shape, dtype):
        n = 1
        for d in shape[1:]:
            n *= d
        nb = n * (2 if dtype == F32 else 1)
        nb = (nb + 15) // 16 * 16
        assert self.cur + nb <= self.n, ("arena overflow", name, self.cur, nb, self.n)
        ap = self.t[:, self.cur:self.cur + nb]
        self.cur += nb
        if dtype == F32:
            ap = ap.bitcast(F32)
        ap = ap[:, 0:n]
        if len(shape) == 3:
            ap = ap.rearrange("p (a b) -> p a b", a=shape[1])
        elif len(shape) == 4:
            ap = ap.rearrange("p (a b c) -> p a b c", a=shape[1], b=shape[2])
        return ap

    def mark(self):
        self.base = self.cur

    def reset(self):
        self.cur = self.base


def common_setup(nc, P, names):
    c = Ctx()
    c.nc = nc
    c.P = P
    c.arena = Arena(nc)
    A = c.arena.alloc
    c.psall_t = nc.alloc_psum_tensor("psall", [128, 4096], F32)
    c.psall = c.psall_t
    c.ps = [c.psall_t[:, i * 512:(i + 1) * 512] for i in range(8)]
    c.psb = [c.psall_t.bitcast(BF16)[:, i * 1024:(i + 1) * 1024] for i in range(8)]
    c.t_ps_single = toks(8)
    c.t_ps = c.t_ps_single
    c.t_pspair = toks(4)
    c.ident_d = nc.dram_tensor("ident", [128, 128], BF16, kind="ExternalInput").ap()
    c.ident = A("ident_sb", [128, 128], BF16)
    c.t_ident = Tok()
    P.dma("sp", lambda e: e.dma_start(out=c.ident[:], in_=c.ident_d), writes=[c.t_ident])
    c.wstage = [A("wstage%d" % i, [128, 8, 128], F32) for i in range(2)]
    c.t_wstage = toks(2)
    c.wstage_i = 0
    c.arena.mark()
    return c


def phase1(c, hprev, p0, p1, nw, hout, hnT, t_hnT, nparts=2, scr=None, aux=None, ntiles=NT):
    nc, P = c.nc, c.P
    A = c.arena.alloc
    if scr is None:
        scr = A("p1scr", [128, 6 * D], F32)
    xt = [scr[:, i * D:(i + 1) * D] for i in range(2)]
    pa = [scr[:, (2 + i) * D:(3 + i) * D] for i in range(2)]
    pb = [scr[:, (4 + i) * D:(5 + i) * D] for i in range(2)]
    if aux is None:
        aux = A("p1aux", [128, 5 * D], BF16)
    hn = [aux[:, i * D:(i + 1) * D] for i in range(2)]
    sq = aux[:, 2 * D:3 * D]
    nwb = aux[:, 3 * D:5 * D].bitcast(F32)
    ssq = [A("p1ssq%d" % i, [128, 1], F32) for i in range(2)]
    rstd = [A("p1rstd%d" % i, [128, 1], F32) for i in range(2)]
    t_xt, t_pa, t_pb, t_hn, t_ssq, t_rstd = toks(2), toks(2), toks(2), toks(2), toks(2), toks(2)
    t_sq, t_nwb = Tok(), Tok()
    P.dma("sp", lambda e: e.dma_start(out=nwb[:], in_=nw.broadcast_to([128, D])), writes=[t_nwb])
    for t in range(ntiles):
        b = t % 2
        rows = slice(t * 128, (t + 1) * 128)
        P.dma("sp", lambda e, b=b, rows=rows: e.dma_start(out=xt[b][:], in_=hprev[rows, :]), writes=[t_xt[b]])
        if p0 is not None:
            P.dma("act", lambda e, b=b, rows=rows: e.dma_start(out=pa[b][:], in_=p0[rows, :]), writes=[t_pa[b]])
            if p1 is not None:
                P.dma("act", lambda e, b=b, rows=rows: e.dma_start(out=pb[b][:], in_=p1[rows, :]), writes=[t_pb[b]])
                P.op("pool", lambda e, b=b: e.tensor_tensor(out=pa[b][:], in0=pa[b][:], in1=pb[b][:], op=ALU.add),
                     reads=[t_pa[b], t_pb[b]], writes=[t_pa[b]])
            P.op("dve", lambda e, b=b: e.tensor_tensor(out=xt[b][:], in0=xt[b][:], in1=pa[b][:], op=ALU.add),
                 reads=[t_xt[b], t_pa[b]], writes=[t_xt[b]])
        if hout is not None and hnT is not None:
            P.dma("sp", lambda e, b=b, rows=rows: e.dma_start(out=hout[rows, :], in_=xt[b][:]), reads=[t_xt[b]])
        P.op("act", lambda e, b=b: e.activation(out=sq[:], in_=xt[b][:], func=AF.Square, accum_out=ssq[b][:]),
             reads=[t_xt[b]], writes=[t_sq, t_ssq[b]])
        P.op("act", lambda e, b=b: e.activation(out=rstd[b][:], in_=ssq[b][:], func=AF.Sqrt, scale=1.0 / D, bias=EPS),
             reads=[t_ssq[b]], writes=[t_rstd[b]])
        P.op("dve", lambda e, b=b: e.reciprocal(out=rstd[b][:], in_=rstd[b][:]), reads=[t_rstd[b]], writes=[t_rstd[b]])
        if hnT is None:
            P.op("dve", lambda e, b=b: e.scalar_tensor_tensor(out=pa[b][:], in0=xt[b][:], scalar=rstd[b][:], in1=nwb[:],
                                                             op0=ALU.mult, op1=ALU.mult),
                 reads=[t_xt[b], t_rstd[b], t_nwb], writes=[t_pa[b]])
            P.dma("sp", lambda e, b=b, rows=rows: e.dma_start(out=hout[rows, :], in_=pa[b][:]), reads=[t_pa[b]])
            continue
        P.op("dve", lambda e, b=b: e.scalar_tensor_tensor(out=hn[b][:], in0=xt[b][:], scalar=rstd[b][:], in1=nwb[:],
                                                         op0=ALU.mult, op1=ALU.mult),
             reads=[t_xt[b], t_rstd[b], t_nwb], writes=[t_hn[b]])
        k = t % 2
        for ch in range(8):
            P.op("pe", lambda e, b=b, ch=ch, k=k: e.transpose(out=c.psb[k][:, ch * 128:(ch + 1) * 128],
                                                             in_=hn[b][:, ch * 128:(ch + 1) * 128], identity=c.ident[:]),
                 reads=[t_hn[b], c.t_ident], writes=[c.t_ps[k]])
        P.op("act", lambda e, t=t, k=k: e.activation(out=hnT[:, :, t * 128:(t + 1) * 128],
                                                    in_=c.psb[k][:, :].rearrange("p (c n) -> p c n", c=8), func=AF.Copy),
             reads=[c.t_ps[k]], writes=[t_hnT])


def load_w(c, wdram, col0, ncols, dst, t_dst, row0=0, nch=8):
    for n0 in range(0, ncols, 128):
        k = c.wstage_i % 2
        c.wstage_i += 1
        st, t_st = c.wstage[k], c.t_wstage[k]
        c.P.dma("sp", lambda e, st=st, n0=n0: e.dma_start(
            out=st[:, 0:nch, :], in_=wdram[row0:row0 + nch * 128, col0 + n0:col0 + n0 + 128].rearrange("(c p) n -> p c n", p=128)),
            writes=[t_st])
        c.P.op("pool", lambda e, st=st, n0=n0: e.tensor_copy(out=dst[:, 0:nch, n0:n0 + 128], in_=st[:, 0:nch, :]),
               reads=[t_st], writes=[t_dst])


def proj_fm(c, wt, t_w, hnT, t_hnT, tb, bank, wtoks=()):
    for ch in range(8):
        c.P.op("pe", lambda e, ch=ch: e.matmul(c.ps[bank][:, :], lhsT=wt[:, ch, :], rhs=hnT[:, ch, tb * 512:(tb + 1) * 512],
                                              start=(ch == 0), stop=(ch == 7)),
               reads=[t_w, t_hnT], writes=[c.t_ps[bank], *wtoks])


def outproj(c, mixT, t_mixT, wout, pout, region, t_pout=None):
    nc, P = c.nc, c.P
    A = c.arena.alloc
    wo = region[:, 0:8192].rearrange("p (c n) -> p c n", c=8)
    t_wo = Tok()
    load_w(c, wout, 0, D, wo, t_wo)
    mt = [region[:, 8192 + i * 4096:8192 + (i + 1) * 4096].rearrange("p (c n) -> p c n", c=8) for i in range(2)]
    ot = [region[:, 16384 + i * 2048:16384 + (i + 1) * 2048].bitcast(F32) for i in range(2)]
    t_mt, t_ot = toks(2), toks(2)
    for tb in range(S // 512):
        mb = tb % 2
        P.dma("sp", lambda e, tb=tb, mb=mb: e.dma_start(out=mt[mb][:], in_=mixT[:, :, tb * 512:(tb + 1) * 512].rearrange("c p n -> p c n")),
              writes=[t_mt[mb]])
        for tt in range(4):
            t = tb * 4 + tt
            ob = t % 2
            for hf in range(2):
                bank = (2 * t + hf) % 4
                for ch in range(8):
                    P.op("pe", lambda e, ch=ch, hf=hf, bank=bank, mb=mb, tt=tt: e.matmul(
                        c.ps[bank][:, :], lhsT=mt[mb][:, ch, tt * 128:(tt + 1) * 128], rhs=wo[:, ch, hf * 512:(hf + 1) * 512],
                        start=(ch == 0), stop=(ch == 7)), reads=[t_mt[mb], t_wo], writes=[c.t_ps[bank]])
                if hf == 0:
                    P.op("act", lambda e, ob=ob, bank=bank: e.activation(out=ot[ob][:, 0:512], in_=c.ps[bank][:, :], func=AF.Copy),
                         reads=[c.t_ps[bank]], writes=[t_ot[ob]])
                else:
                    P.op("dve", lambda e, ob=ob, bank=bank: e.tensor_copy(out=ot[ob][:, 512:1024], in_=c.ps[bank][:, :]),
                         reads=[c.t_ps[bank]], writes=[t_ot[ob]])
            P.dma("sp", lambda e, t=t, ob=ob: e.dma_start(out=pout[t * 128:(t + 1) * 128, :], in_=ot[ob][:]),
                  reads=[t_ot[ob]] + ([t_pout[t // 8]] if t_pout is not None else []))


A_PATTERNS = (1, 4, 16)


def build_even(stage=99, npairs=4, patterns=(1, 4, 16), nheadsB=4, skip=(), dbg=False, fused=None):
    if fused is None:
        nc = bass.Bass("TRN2", target_bir_lowering=False)
        P = Prog(nc)
        pre = ""
    else:
        nc, P, pre = fused["nc"], fused["P"], fused["pre"]
    dram = lambda n, shp, dt=F32: nc.dram_tensor(pre + n, shp, dt, kind="ExternalInput").ap()
    if fused is None:
        hprev = dram("hprev", [S, D])
        p0 = dram("p0", [S, D])
        p1 = dram("p1", [S, D])
    else:
        hprev, p0, p1 = fused["hprev"], fused["psum"], None
    nw = dram("nw", [1, D])
    win = dram("win", [D, 4096])
    wout = dram("wout", [D, D])
    lbl = dram("lbl", [128, 8])
    lbsel = dram("lbsel", [128, 1])
    hnw = dram("hnw", [128, 1])
    maskA_d = dram("maskA", [128, 512], BF16)
    onese_d = dram("onese", [128, 256], BF16)
    maskB_d = dram("maskB", [128, 128], BF16)
    rowmask_d = dram("rowmask", [128, 2])
    resetm_d = dram("resetm", [128, 512])
    if fused is None:
        hout = nc.dram_tensor("hout", [S, D], F32, kind="ExternalOutput").ap()
        pout = nc.dram_tensor("pout", [S, D], F32, kind="ExternalOutput").ap()
        mixT = nc.dram_tensor("mixT", [8, 128, S], BF16, kind=("ExternalOutput" if dbg else "Internal")).ap()
        c = common_setup(nc, P, None)
    else:
        hout, pout, mixT, c = fused["hout"], fused["pout"], fused["mixT"], fused["c"]
        c.arena.reset()
    c.t_ps = c.t_ps_single
    t_mixT = Tok()
    A = c.arena.alloc
    hnT = A("hnT", [128, 8, S], BF16)
    t_hnT = Tok()

    scr = A("scr", [128, 2 * S], F32)
    phase1(c, hprev, p0, p1, nw, hout, hnT, t_hnT, scr=scr)
    P.barrier()
    if stage <= 1:
        P.emit()
        return nc

    maskA = A("maskA_sb", [128, 512], BF16)
    onese = A("onese_sb", [128, 2, 128], BF16)
    maskB = A("maskB_sb", [128, 128], BF16)
    rowmask = A("rowmask_sb", [128, 2], F32)
    resetm = A("resetm_sb", [128, 512], F32)
    onesb = A("onesb", [128, 128], BF16)
    lbt = A("lbt", [128, 8], F32)
    lbs = A("lbs", [128, 1], F32)
    lb = A("lb", [128, 4], F32)
    oml = A("oml", [128, 4], F32)
    hnws = A("hnws", [128, 1], F32)
    t_const = Tok()
    for dst, src in ((maskA[:], maskA_d), (onese[:], onese_d.rearrange("p (a b) -> p a b", a=2)), (maskB[:], maskB_d),
                     (resetm[:], resetm_d), (rowmask[:], rowmask_d), (lbt[:], lbl), (lbs[:], lbsel), (hnws[:], hnw)):
        P.dma("sp", lambda e, dst=dst, src=src: e.dma_start(out=dst, in_=src), writes=[t_const])
    P.op("pool", lambda e: e.memset(onesb[:], 1.0), writes=[t_const])
    P.op("dve", lambda e: e.tensor_tensor(out=lb[:], in0=lbt[:, 4:8], in1=lbt[:, 0:4], op=ALU.subtract), reads=[t_const], writes=[t_const])
    P.op("act", lambda e: e.activation(out=lb[:], in_=lb[:], func=AF.Sigmoid), reads=[t_const], writes=[t_const])
    P.op("dve", lambda e: e.tensor_scalar(out=lb[:], in0=lb[:], scalar1=lbs[:], scalar2=None, op0=ALU.mult), reads=[t_const], writes=[t_const])
    P.op("dve", lambda e: e.tensor_scalar(out=oml[:], in0=lb[:], scalar1=-1.0, scalar2=1.0, op0=ALU.mult, op1=ALU.add),
         reads=[t_const], writes=[t_const])

    wq = [A("wq%d" % i, [128, 8, 128], BF16) for i in range(2)]
    wk = [A("wk%d" % i, [128, 8, 128], BF16) for i in range(2)]
    wv = [A("wv%d" % i, [128, 8, 128], BF16) for i in range(2)]
    wg = [A("wg%d" % i, [128, 8, 128], BF16) for i in range(2)]
    t_wq, t_wk, t_wv, t_wg = toks(2), toks(2), toks(2), toks(2)
    regA = A("regA", [128, 5 * S], BF16)
    QT, KT, GT = regA[:, 0:S], regA[:, S:2 * S], regA[:, 2 * S:3 * S]
    QTo = regA[:, 4 * S:5 * S]
    QTs = (QT, QTo)
    VT = A("VT", [128, S], BF16)
    t_VT = Tok()
    t_QT, t_KT, t_GT = Tok(), Tok(), Tok()
    acc = scr[:, :].rearrange("p (a n) -> p a n", a=2)
    t_acc = Tok()
    NV = 4
    NPT = 4
    Vz = [A("Vz%d" % i, [128, 2, 128], BF16) for i in range(NV)]
    t_Vz = toks(NV)
    PT = [A("PT%d" % i, [128, 2, 256], BF16) for i in range(NPT)]
    t_PT = toks(NPT)
    mixA = regA[:, 3 * S:4 * S]
    t_mixA = Tok()
    for i in range(NV):
        P.op("pool", lambda e, i=i: e.memset(Vz[i][:], 0.0), writes=[t_Vz[i]])

    kb_count = 0
    for p in range(npairs):
        wb = p % 2
        load_w(c, win, 0 * 512 + p * 128, 128, wq[wb], t_wq[wb])
        load_w(c, win, 1 * 512 + p * 128, 128, wk[wb], t_wk[wb])
        load_w(c, win, 2 * 512 + p * 128, 128, wv[wb], t_wv[wb])
        load_w(c, win, 3 * 512 + p * 128, 128, wg[wb], t_wg[wb])
        for tb in range(8):
            cols = slice(tb * 512, (tb + 1) * 512)
            bq, bk, bg = (3 * tb) % 4, (3 * tb + 1) % 4, (3 * tb + 2) % 4
            proj_fm(c, wq[wb], t_wq[wb], hnT, t_hnT, tb, bq)
            P.op("act", lambda e, cols=cols, bq=bq: e.activation(out=QT[:, cols], in_=c.ps[bq][:, :], func=AF.Copy, scale=rowmask[:, 0:1]),
                 reads=[c.t_ps[bq], t_const], writes=[t_QT])
            P.op("dve", lambda e, cols=cols, bq=bq: e.tensor_scalar(out=QTo[:, cols], in0=c.ps[bq][:, :], scalar1=rowmask[:, 1:2], scalar2=None,
                                                                   op0=ALU.mult), reads=[c.t_ps[bq], t_const], writes=[t_QT])
            proj_fm(c, wk[wb], t_wk[wb], hnT, t_hnT, tb, bk)
            P.op("dve", lambda e, cols=cols, bk=bk: e.tensor_copy(out=KT[:, cols], in_=c.ps[bk][:, :]),
                 reads=[c.t_ps[bk]], writes=[t_KT])
            proj_fm(c, wg[wb], t_wg[wb], hnT, t_hnT, tb, bg)
            P.op("act", lambda e, cols=cols, bg=bg: e.activation(out=GT[:, cols], in_=c.ps[bg][:, :], func=AF.Silu),
                 reads=[c.t_ps[bg]], writes=[t_GT])
            bvv = (3 * tb + 3) % 4
            proj_fm(c, wv[wb], t_wv[wb], hnT, t_hnT, tb, bvv)
            P.op("dve", lambda e, cols=cols, bvv=bvv: e.tensor_copy(out=VT[:, cols], in_=c.ps[bvv][:, :]),
                 reads=[c.t_ps[bvv]], writes=[t_VT])
        P.op("pool", lambda e: e.memset(acc[:], 0.0), writes=[t_acc])
        for d in (() if 'attn' in skip else patterns):
            nb = S // (128 * d)
            for r in range(d):
                for n in range(nb):
                    nq = 256 if n < nb - 1 else 128
                    k0 = 128 * n * d + r
                    keys = sl(k0, 128, d)
                    qs = sl(k0, nq, d)
                    vi = kb_count % NV
                    pi = kb_count % NPT
                    bv = (4, 0)[kb_count % 2]
                    bs = (5, 6, 1)[kb_count % 3]
                    bn = (7, 3, 2)[kb_count % 3]
                    kb_count += 1
                    P.op("pe", lambda e, keys=keys, bv=bv: e.transpose(out=c.psb[bv][:, 0:128], in_=VT[:, keys], identity=c.ident[:]),
                         reads=[t_VT, c.t_ident], writes=[c.t_ps[bv]])
                    P.op("dve", lambda e, vi=vi, bv=bv: e.tensor_copy(out=Vz[vi][:, 0, 0:64], in_=c.psb[bv][:, 0:64]),
                         reads=[c.t_ps[bv]], writes=[t_Vz[vi]])
                    P.op("dve", lambda e, vi=vi, bv=bv: e.tensor_copy(out=Vz[vi][:, 1, 64:128], in_=c.psb[bv][:, 64:128]),
                         reads=[c.t_ps[bv]], writes=[t_Vz[vi]])
                    st3 = c.ps[bs][:, :].rearrange("p (a n) -> p a n", a=2)
                    for hh in range(2):
                        P.op("pe", lambda e, hh=hh, keys=keys, qs=qs, nq=nq, st3=st3: e.matmul(
                            st3[:, hh, 0:nq], lhsT=KT[:, keys], rhs=QTs[hh][:, qs], start=True, stop=True),
                            reads=[t_KT, t_QT], writes=[c.t_ps[bs]])
                    P.op("act", lambda e, pi=pi, nq=nq, st3=st3: e.activation(out=PT[pi][:, :, 0:nq], in_=st3[:, :, 0:nq],
                                                                             func=AF.Exp, scale=0.125),
                         reads=[c.t_ps[bs]], writes=[t_PT[pi]])
                    m3 = maskA[:, :].rearrange("p (a n) -> p a n", a=2)
                    P.op("pool", lambda e, pi=pi, nq=nq, m3=m3: e.tensor_tensor(out=PT[pi][:, :, 0:nq], in0=PT[pi][:, :, 0:nq],
                                                                               in1=m3[:, :, 0:nq], op=ALU.mult),
                         reads=[t_PT[pi], t_const], writes=[t_PT[pi]])
                    nd3 = c.ps[bn][:, :].rearrange("p (a n) -> p a n", a=2)
                    for which, lh in ((0, Vz[vi]), (1, onese)):
                        for hh in range(2):
                            P.op("pe", lambda e, which=which, lh=lh, hh=hh, pi=pi, nq=nq, nd3=nd3: e.matmul(
                                nd3[:, which, 0:nq], lhsT=lh[:, hh, :], rhs=PT[pi][:, hh, 0:nq], start=(hh == 0), stop=(hh == 1)),
                                reads=[t_Vz[vi], t_const, t_PT[pi]], writes=[c.t_ps[bn]])
                    P.op("dve", lambda e, qs=qs, nq=nq, nd3=nd3: e.tensor_tensor(out=acc[:, :, qs], in0=acc[:, :, qs],
                                                                                in1=nd3[:, :, 0:nq], op=ALU.add),
                         reads=[c.t_ps[bn], t_acc], writes=[t_acc])
        for q4 in (() if 'fin' in skip else range(4)):
            cols = slice(q4 * 1024, (q4 + 1) * 1024)
            P.op("dve", lambda e, cols=cols: e.reciprocal(out=acc[:, 1, cols], in_=acc[:, 1, cols]), reads=[t_acc], writes=[t_acc])
            P.op("dve", lambda e, cols=cols: e.tensor_tensor(out=acc[:, 0, cols], in0=acc[:, 0, cols], in1=acc[:, 1, cols], op=ALU.mult),
                 reads=[t_acc], writes=[t_acc])
            P.op("pool", lambda e, cols=cols: e.tensor_tensor(out=mixA[:, cols], in0=acc[:, 0, cols], in1=GT[:, cols], op=ALU.mult),
                 reads=[t_acc, t_GT], writes=[t_mixA])
        if 'mixdma' not in skip:
            P.dma("sp", lambda e, p=p: e.dma_start(out=mixT[p, :, :], in_=mixA[:, :]), reads=[t_mixA])

    P.barrier()
    if stage <= 2:
        nheadsB = 0
    wbq, wbf, wbi, wbg = wq, wk, wv, wg
    t_wbq, t_wbf, t_wbi, t_wbg = t_wq, t_wk, t_wv, t_wg
    _cur = [0]

    def carve(nbf16):
        a = regA[:, _cur[0]:_cur[0] + nbf16]
        _cur[0] += nbf16
        return a
    f32t = lambda n: carve(1024).bitcast(F32)
    sig, kTt, lf, Gc, Gx, ex = f32t("b_sig"), f32t("b_kT"), f32t("b_lf"), f32t("b_Gc"), f32t("b_Gx"), f32t("b_ex")
    t_sig, t_kT, t_lf, t_Gc, t_Gx, t_ex = toks(6)
    ex2, ex3, ex4, Gx2 = f32t("b_ex2"), f32t("b_ex3"), f32t("b_ex4"), f32t("b_Gx2")
    t_ex2, t_ex3, t_ex4, t_Gx2 = toks(4)
    bft = lambda n: carve(512)
    GTb, qt, kt, qd, ktl, sqb, mixB = bft("b_GT"), bft("b_qt"), bft("b_kt"), bft("b_qd"), bft("b_ktl"), bft("b_sq"), bft("b_mix")
    t_GTb, t_qt, t_kt, t_qd, t_ktl, t_sqb, t_mixB = toks(7)
    eg = A("b_eg", [128, 8], F32)
    t_eg = Tok()
    kt_tok = A("b_kttok", [128, 2, 4, 128], BF16)
    v_tok = A("b_vtok", [128, 4, 128], BF16)
    t_kttok, t_vtok = Tok(), Tok()
    at = A("b_at", [128, 128], BF16)
    t_at = Tok()
    St2 = [A("b_S0", [128, 128], F32), A("b_S1", [128, 128], F32)]
    St = St2[0]
    Sb = A("b_Sb", [128, 128], BF16)
    t_S, t_Sb = Tok(), Tok()
    rs = carve(1024).bitcast(F32)
    t_rs = Tok()
    Gc3 = Gc[:, :].rearrange("p (c n) -> p c n", n=64)
    for hb in range(nheadsB):
        wb = hb % 2
        load_w(c, win, 4 * 512 + hb * 128, 128, wbq[wb], t_wbq[wb])
        load_w(c, win, 5 * 512 + hb * 128, 128, wbf[wb], t_wbf[wb])
        load_w(c, win, 6 * 512 + hb * 128, 128, wbi[wb], t_wbi[wb])
        load_w(c, win, 7 * 512 + hb * 128, 128, wbg[wb], t_wbg[wb])
        P.op("pool", lambda e: e.memset(St2[0][:], 0.0), writes=[t_S])
        P.op("pool", lambda e: e.memset(St2[1][:], 0.0), writes=[t_S])
        s_cur = 0
        P.op("pool", lambda e: e.memset(Sb[:], 0.0), writes=[t_Sb])
        for tb in range(8):
            BQ, BF_, BG, BV, BA, BD, BO, BS = 0, 1, 2, 3, 4, 5, 6, 7
            proj_fm(c, wbq[wb], t_wbq[wb], hnT, t_hnT, tb, BQ)
            proj_fm(c, wbf[wb], t_wbf[wb], hnT, t_hnT, tb, BF_)
            proj_fm(c, wbg[wb], t_wbg[wb], hnT, t_hnT, tb, BG)
            P.op("act", lambda e: e.activation(out=GTb[:], in_=c.ps[BG][:, :], func=AF.Silu), reads=[c.t_ps[BG]], writes=[t_GTb])
            P.op("act", lambda e: e.activation(out=sig[:], in_=c.ps[BF_][:, :], func=AF.Sigmoid), reads=[c.t_ps[BF_]], writes=[t_sig])
            P.op("dve", lambda e, hb=hb: e.tensor_scalar(out=sig[:], in0=sig[:], scalar1=oml[:, hb:hb + 1], scalar2=lb[:, hb:hb + 1],
                                                        op0=ALU.mult, op1=ALU.add), reads=[t_sig, t_const], writes=[t_sig])
            P.op("pool", lambda e: e.tensor_scalar(out=kTt[:], in0=sig[:], scalar1=-1.0, scalar2=1.0, op0=ALU.mult, op1=ALU.add),
                 reads=[t_sig], writes=[t_kT])
            P.op("act", lambda e: e.activation(out=lf[:], in_=sig[:], func=AF.Ln), reads=[t_sig], writes=[t_lf])
            P.op("dve", lambda e: e.tensor_tensor_scan(out=Gc[:], data0=resetm[:], data1=lf[:], initial=0.0, op0=ALU.mult, op1=ALU.add),
                 reads=[t_lf, t_const], writes=[t_Gc])
            P.op("dve", lambda e: e.tensor_tensor(out=Gx[:, :].rearrange("p (c n) -> p c n", n=64), in0=Gc3,
                                                 in1=Gc3[:, :, 31:32].broadcast_to([128, 8, 64]), op=ALU.subtract),
                 reads=[t_Gc], writes=[t_Gx])
            P.op("dve", lambda e: e.tensor_scalar(out=Gx[:], in0=Gx[:], scalar1=-40.0, scalar2=40.0, op0=ALU.max, op1=ALU.min),
                 reads=[t_Gx], writes=[t_Gx])
            P.op("act", lambda e: e.activation(out=ex[:], in_=Gx[:], func=AF.Exp), reads=[t_Gx], writes=[t_ex])
            P.op("dve", lambda e: e.tensor_tensor(out=qt[:], in0=c.ps[BQ][:, :], in1=ex[:], op=ALU.mult),
                 reads=[c.t_ps[BQ], t_ex], writes=[t_qt])
            P.op("act", lambda e: e.activation(out=ex2[:], in_=Gx[:], func=AF.Exp, scale=-1.0), reads=[t_Gx], writes=[t_ex2])
            P.op("pool", lambda e: e.tensor_tensor(out=kt[:], in0=kTt[:], in1=ex2[:], op=ALU.mult), reads=[t_kT, t_ex2], writes=[t_kt])
            P.op("act", lambda e: e.activation(out=ex3[:], in_=Gc[:], func=AF.Exp), reads=[t_Gc], writes=[t_ex3])
            P.op("dve", lambda e: e.tensor_tensor(out=qd[:], in0=c.ps[BQ][:, :], in1=ex3[:], op=ALU.mult),
                 reads=[c.t_ps[BQ], t_ex3], writes=[t_qd])
            P.op("dve", lambda e: e.tensor_tensor(out=Gx2[:, :].rearrange("p (c n) -> p c n", n=64),
                                                 in0=Gc3[:, :, 63:64].broadcast_to([128, 8, 64]), in1=Gc3, op=ALU.subtract),
                 reads=[t_Gc], writes=[t_Gx2])
            P.op("act", lambda e: e.activation(out=ex4[:], in_=Gx2[:], func=AF.Exp), reads=[t_Gx2], writes=[t_ex4])
            P.op("pool", lambda e: e.tensor_tensor(out=ktl[:], in0=kTt[:], in1=ex4[:], op=ALU.mult), reads=[t_kT, t_ex4], writes=[t_ktl])
            P.op("act", lambda e: e.activation(out=eg[:], in_=Gc[:, sl(63, 8, 64)], func=AF.Exp), reads=[t_Gc], writes=[t_eg])
            for tt in range(4):
                P.op("pe", lambda e, tt=tt: e.transpose(out=c.psb[BV][:, tt * 128:(tt + 1) * 128], in_=ktl[:, tt * 128:(tt + 1) * 128],
                                                       identity=c.ident[:]), reads=[t_ktl, c.t_ident], writes=[c.t_ps[BV]])
            P.op("act", lambda e: e.activation(out=kt_tok[:, 0, :, :], in_=c.psb[BV][:, 0:512].rearrange("p (a n) -> p a n", a=4), func=AF.Copy,
                                               scale=rowmask[:, 0:1]), reads=[c.t_ps[BV], t_const], writes=[t_kttok])
            P.op("dve", lambda e: e.tensor_scalar(out=kt_tok[:, 1, :, :], in0=c.psb[BV][:, 0:512].rearrange("p (a n) -> p a n", a=4),
                                                 scalar1=rowmask[:, 1:2], scalar2=None, op0=ALU.mult),
                 reads=[c.t_ps[BV], t_const], writes=[t_kttok])
            for tt in range(4):
                tk = slice(tb * 512 + tt * 128, tb * 512 + (tt + 1) * 128)
                for ch in range(8):
                    P.op("pe", lambda e, tt=tt, tk=tk, ch=ch, wb=wb: e.matmul(c.ps[BV][:, tt * 128:(tt + 1) * 128], lhsT=hnT[:, ch, tk],
                                                                             rhs=wbi[wb][:, ch, :], start=(ch == 0), stop=(ch == 7)),
                         reads=[t_hnT, t_wbi[wb]], writes=[c.t_ps[BV]])
            P.op("act", lambda e: e.activation(out=v_tok[:, :, :], in_=c.ps[BV][:, :].rearrange("p (a n) -> p a n", a=4), func=AF.Copy),
                 reads=[c.t_ps[BV]], writes=[t_vtok])
            for tt in range(4):
                tk = slice(tt * 128, (tt + 1) * 128)
                P.op("pe", lambda e, tk=tk: e.matmul(c.ps[BA][:, 0:128], lhsT=kt[:, tk], rhs=qt[:, tk], start=True, stop=True),
                     reads=[t_kt, t_qt], writes=[c.t_ps[BA]])
                P.op("dve", lambda e: e.tensor_tensor(out=at[:, :], in0=c.ps[BA][:, 0:128], in1=maskB[:, :], op=ALU.mult),
                     reads=[c.t_ps[BA], t_const], writes=[t_at])
                for cc in range(2):
                    rr = slice(cc * 64, (cc + 1) * 64)
                    oc = slice(tt * 128 + cc * 64, tt * 128 + (cc + 1) * 64)
                    P.op("pe", lambda e, cc=cc, oc=oc, tt=tt: e.matmul(c.ps[BO][:, oc], lhsT=v_tok[:, tt, :], rhs=at[:, cc * 64:(cc + 1) * 64],
                                                                      start=True, stop=False),
                         reads=[t_vtok, t_at], writes=[c.t_ps[BO]])
                    P.op("pe", lambda e, oc=oc: e.matmul(c.ps[BO][:, oc], lhsT=Sb[:, :], rhs=qd[:, oc], start=False, stop=True),
                         reads=[t_Sb, t_qd], writes=[c.t_ps[BO]])
                    P.op("pe", lambda e, cc=cc, tt=tt: e.matmul(c.ps[BD][:, 0:128], lhsT=kt_tok[:, cc, tt, :], rhs=v_tok[:, tt, :],
                                                               start=True, stop=True),
                         reads=[t_kttok, t_vtok], writes=[c.t_ps[BD]])
                    ci = tt * 2 + cc
                    So, Sn = St2[s_cur], St2[1 - s_cur]
                    s_cur = 1 - s_cur
                    P.op("dve", lambda e, ci=ci, So=So: e.scalar_tensor_tensor(out=Sb[:], in0=So[:], scalar=eg[:, ci:ci + 1], in1=c.ps[BD][:, 0:128],
                                                                              op0=ALU.mult, op1=ALU.add),
                         reads=[t_S, t_eg, c.t_ps[BD]], writes=[t_Sb])
                    P.op("dve", lambda e, ci=ci, So=So, Sn=Sn: e.scalar_tensor_tensor(out=Sn[:], in0=So[:], scalar=eg[:, ci:ci + 1], in1=c.ps[BD][:, 0:128],
                                                                                     op0=ALU.mult, op1=ALU.add),
                         reads=[t_S, t_eg, c.t_ps[BD]], writes=[t_S])
            P.op("act", lambda e: e.activation(out=sqb[:], in_=c.ps[BO][:, :], func=AF.Square), reads=[c.t_ps[BO]], writes=[t_sqb])
            P.op("pe", lambda e: e.matmul(c.ps[BS][:, :], lhsT=onesb[:, :], rhs=sqb[:], start=True, stop=True),
                 reads=[t_sqb, t_const], writes=[c.t_ps[BS]])
            P.op("act", lambda e: e.activation(out=rs[:], in_=c.ps[BS][:, :], func=AF.Sqrt, scale=1.0 / 128, bias=EPS),
                 reads=[c.t_ps[BS]], writes=[t_rs])
            P.op("dve", lambda e: e.reciprocal(out=rs[:], in_=rs[:]), reads=[t_rs], writes=[t_rs])
            P.op("dve", lambda e: e.tensor_tensor(out=rs[:], in0=c.ps[BO][:, :], in1=rs[:], op=ALU.mult),
                 reads=[c.t_ps[BO], t_rs], writes=[t_rs])
            P.op("dve", lambda e: e.scalar_tensor_tensor(out=mixB[:], in0=rs[:], scalar=hnws[:, 0:1], in1=GTb[:], op0=ALU.mult, op1=ALU.mult),
                 reads=[t_rs, t_GTb, t_const], writes=[t_mixB])
            P.dma("sp", lambda e, hb=hb, tb=tb: e.dma_start(out=mixT[4 + hb, :, tb * 512:(tb + 1) * 512], in_=mixB[:, :]),
                  reads=[t_mixB])

    if 'outproj' in skip:
        P.emit()
        return nc
    if stage < 99:
        for k in list(range(npairs, 4)) + list(range(4 + nheadsB, 8)):
            P.dma("sp", lambda e, k=k: e.dma_start(out=mixT[k, :, :], in_=mixA[:, :]))
    P.barrier()
    outproj(c, mixT, t_mixT, wout, pout, hnT[:, :, :].rearrange("p c n -> p (c n)"), t_pout=(fused or {}).get("t_pout"))
    if fused is None:
        P.emit()
    return nc


def even_consts():
    bf = ml_dtypes.bfloat16
    k = np.arange(128)[:, None]
    q = np.arange(128)[None, :]
    half = np.concatenate([(q >= k), (k >= q)], axis=1).astype(np.float32)
    maskA = np.concatenate([half, half], axis=1).astype(bf)
    onese = np.zeros((128, 2, 128), np.float32)
    onese[:, 0, 0:64] = 1.0
    onese[:, 1, 64:128] = 1.0
    j = np.arange(64)[:, None]
    i = np.arange(64)[None, :]
    mb = (i >= j).astype(np.float32)
    maskB = np.zeros((128, 128), np.float32)
    maskB[0:64, 0:64] = mb
    maskB[64:128, 64:128] = mb
    maskB = maskB.astype(bf)
    rowmask = np.zeros((128, 2), np.float32)
    rowmask[0:64, 0] = 1.0
    rowmask[64:128, 1] = 1.0
    resetm = np.ones((128, 512), np.float32)
    resetm[:, ::64] = 0.0
    return {"maskA": maskA, "onese": onese.reshape(128, 256).astype(bf), "maskB": maskB, "resetm": resetm, "rowmask": rowmask,
            "ident": np.eye(128, dtype=np.float32).astype(bf)}


def even_inputs(hprev, p0, p1, norm_w_l, w_in, w_out, lb_logits, hgrn_nw, jl, j):
    A0, B0 = 0, 4096
    cols = []
    for blk in range(4):
        cols.append(w_in[:, A0 + blk * 1024 + j * 512: A0 + blk * 1024 + (j + 1) * 512])
    for blk in range(4):
        cols.append(w_in[:, B0 + blk * 1024 + j * 512: B0 + blk * 1024 + (j + 1) * 512])
    win = np.ascontiguousarray(np.concatenate(cols, axis=1))
    wout = np.ascontiguousarray(np.concatenate([w_out[j * 512:(j + 1) * 512], w_out[1024 + j * 512:1024 + (j + 1) * 512]], axis=0))
    lbl = lb_logits[:, j * 512:(j + 1) * 512].reshape(2, 4, 128).transpose(2, 0, 1).reshape(128, 8)
    m = {"hprev": hprev, "p0": p0, "p1": p1, "nw": norm_w_l.reshape(1, D), "win": win, "wout": wout,
         "lbl": np.ascontiguousarray(lbl), "lbsel": np.full((128, 1), float(jl), np.float32),
         "hnw": np.ascontiguousarray(hgrn_nw.reshape(128, 1))}
    m.update(even_consts())
    return m


def build_odd(nblocks=8, stage=99, dbg=False, fused=None):
    if fused is None:
        nc = bass.Bass("TRN2", target_bir_lowering=False)
        P = Prog(nc)
        pre = ""
    else:
        nc, P, pre = fused["nc"], fused["P"], fused["pre"]
    dram = lambda n, shp, dt=F32: nc.dram_tensor(pre + n, shp, dt, kind="ExternalInput").ap()
    if fused is None:
        hprev = dram("hprev", [S, D])
        p0 = dram("p0", [S, D])
        p1 = dram("p1", [S, D])
    else:
        hprev, p0, p1 = fused["hprev"], fused["psum"], None
    nw = dram("nw", [1, D])
    win = dram("win", [D, 3072 + 128])
    wout = dram("wout", [D, D])
    cw_d = dram("cw", [128, 64])
    hc_d = dram("hconst", [1, 16])
    onw_d = dram("onw", [128, 1])
    tri_d = dram("tri", [128, 384])
    sel_d = dram("csel", [128, 256])
    mstrict_d = dram("mstrict", [128, 128], BF16)
    mcausal_d = dram("mcausal", [128, 128], BF16)
    delta_d = dram("delta", [128, 8])
    rowmask_d = dram("rowmask", [128, 2])
    if fused is None:
        hout = nc.dram_tensor("hout", [S, D], F32, kind="ExternalOutput").ap()
        pout = nc.dram_tensor("pout", [S, D], F32, kind="ExternalOutput").ap()
        mixT = nc.dram_tensor("mixT", [8, 128, S], BF16, kind=("ExternalOutput" if dbg else "Internal")).ap()
        c = common_setup(nc, P, None)
    else:
        hout, pout, mixT, c = fused["hout"], fused["pout"], fused["mixT"], fused["c"]
        c.arena.reset()
    A = c.arena.alloc
    hnT = A("hnT", [128, 8, S], BF16)
    t_hnT = Tok()
    scr = A("scr", [128, 6 * D], F32)
    c.t_ps = [c.t_pspair[k // 2] for k in range(8)]
    xflat = A("xbuf", [128, 16 * 516], BF16)
    xbuf = xflat[:, :].rearrange("p (a n) -> p a n", a=16)
    phase1(c, hprev, p0, p1, nw, hout, hnT, t_hnT, scr=scr, aux=xflat[:, :])
    P.barrier()
    if stage <= 1:
        P.emit()
        return nc

    t_const = Tok()
    cw = A("cw_sb", [128, 64], F32)
    hcb = A("hcb", [128, 16], F32)
    onw = A("onw_sb", [128, 1], F32)
    tri = A("tri_sb", [128, 256], F32)
    csel = A("csel_sb", [128, 256], F32)
    mstrict = A("mstrict_sb", [128, 128], BF16)
    mcausal = A("mcausal_sb", [128, 128], BF16)
    delta = A("delta_sb", [128, 8], F32)
    rowmask = A("rowmask_sb", [128, 2], F32)
    onesb = A("onesb", [128, 128], BF16)
    onesf = A("onesf", [128, 128], F32)
    identf = A("identf", [128, 128], F32)
    negA = A("negA", [128, 8], F32)
    dg = A("dg", [128, 64, 128], BF16)
    for dst, src in ((cw[:], cw_d), (hcb[:], hc_d.broadcast_to([128, 16])), (onw[:], onw_d), (tri[:], tri_d[:, 0:256]),
                     (csel[:], sel_d), (mstrict[:], mstrict_d), (mcausal[:], mcausal_d), (delta[:], delta_d), (rowmask[:], rowmask_d)):
        P.dma("sp", lambda e, dst=dst, src=src: e.dma_start(out=dst, in_=src), writes=[t_const])
    P.op("pool", lambda e: e.memset(onesb[:], 1.0), writes=[t_const])
    P.op("pool", lambda e: e.memset(onesf[:], 1.0), writes=[t_const])
    P.op("dve", lambda e: e.tensor_copy(out=identf[:], in_=c.ident[:]), reads=[c.t_ident], writes=[t_const])
    P.op("act", lambda e: e.activation(out=negA[:], in_=hcb[:, 8:16], func=AF.Exp), reads=[t_const], writes=[t_const])
    P.op("dve", lambda e: e.tensor_scalar(out=negA[:], in0=negA[:], scalar1=-1.0, scalar2=None, op0=ALU.mult), reads=[t_const], writes=[t_const])
    for k in range(64):
        eng = "dve" if k % 2 == 0 else "pool"
        P.op(eng, lambda e, k=k: e.tensor_scalar(out=dg[:, k, :], in0=identf[:], scalar1=cw[:, k:k + 1], scalar2=None, op0=ALU.mult),
             reads=[t_const], writes=[t_const])

    wba = A("wba", [128, 8, 128], BF16)
    t_wba = Tok()
    load_w(c, win, 3072, 128, wba, t_wba)
    beta = A("beta", [128, 1, 8], F32)
    gpad = A("gpad", [128, 128], F32)
    t_beta, t_gpad = Tok(), Tok()
    xa = A("xa", [128, 8], F32)
    t_xa = Tok()
    gc = A("gc", [128, 1, 8], F32)
    gam = A("gam", [128, 1, 8], F32)
    bgam = A("bgam", [128, 1, 8], F32)
    etl = A("etl", [128, 1, 2, 8], F32)
    egl = A("egl", [128, 1, 2, 8], F32)
    gcT = A("gcT", [128, 1, 128], F32)
    t_sc = Tok()
    P.op("pool", lambda e: e.memset(gpad[:], 0.0), writes=[t_gpad])
    def scalars(t):
        tk = slice(t * 128, (t + 1) * 128)
        for ch in range(8):
            P.op("pe", lambda e, ch=ch, tk=tk: e.matmul(c.ps[0][:, 0:128], lhsT=hnT[:, ch, tk], rhs=wba[:, ch, :], start=(ch == 0), stop=(ch == 7)),
                 reads=[t_hnT, t_wba], writes=[c.t_ps[0]])
        P.op("act", lambda e, t=t: e.activation(out=beta[:, 0, :], in_=c.ps[0][:, 0:8], func=AF.Sigmoid), reads=[c.t_ps[0]], writes=[t_beta])
        P.op("dve", lambda e: e.tensor_tensor(out=xa[:], in0=c.ps[0][:, 8:16], in1=hcb[:, 0:8], op=ALU.add),
             reads=[c.t_ps[0], t_const], writes=[t_xa])
        P.op("act", lambda e: e.activation(out=xa[:], in_=xa[:], func=AF.Exp), reads=[t_xa], writes=[t_xa])
        P.op("act", lambda e: e.activation(out=xa[:], in_=xa[:], func=AF.Ln, bias=1.0), reads=[t_xa], writes=[t_xa])
        P.op("dve", lambda e: e.tensor_tensor(out=gpad[:, 0:8], in0=xa[:], in1=negA[:], op=ALU.mult), reads=[t_xa, t_const], writes=[t_gpad])
        P.op("pe", lambda e: e.matmul(c.ps[1][:, 0:8], lhsT=tri[:, 0:128], rhs=gpad[:, 0:8], start=True, stop=True),
             reads=[t_gpad, t_const], writes=[c.t_ps[1]])
        P.op("pe", lambda e: e.matmul(c.ps[1][:, 8:16], lhsT=tri[:, 128:256], rhs=gpad[:, 0:8], start=True, stop=True),
             reads=[t_gpad, t_const], writes=[c.t_ps[1]])
        for cc in range(2):
            P.op("pe", lambda e, cc=cc: e.matmul(c.ps[1][:, 16 + cc * 8:24 + cc * 8], lhsT=csel[:, cc * 128:(cc + 1) * 128], rhs=gpad[:, 0:8],
                                                start=True, stop=True), reads=[t_gpad, t_const], writes=[c.t_ps[1]])
        P.op("pe", lambda e: e.matmul(c.ps[1][:, 128:256], lhsT=gpad[:, :], rhs=tri[:, 0:128], start=True, stop=True),
             reads=[t_gpad, t_const], writes=[c.t_ps[1]])
        P.op("dve", lambda e, t=t: e.tensor_copy(out=gc[:, 0, :], in_=c.ps[1][:, 0:8]), reads=[c.t_ps[1]], writes=[t_sc])
        P.op("act", lambda e, t=t: e.activation(out=gam[:, 0, :], in_=c.ps[1][:, 0:8], func=AF.Exp), reads=[c.t_ps[1]], writes=[t_sc])
        P.op("dve", lambda e, t=t: e.tensor_tensor(out=bgam[:, 0, :], in0=gam[:, 0, :], in1=beta[:, 0, :], op=ALU.mult),
             reads=[t_sc, t_beta], writes=[t_sc])
        P.op("act", lambda e, t=t: e.activation(out=etl[:, 0, 0, :], in_=c.ps[1][:, 8:16], func=AF.Exp), reads=[c.t_ps[1]], writes=[t_sc])
        P.op("dve", lambda e, t=t: e.tensor_scalar(out=etl[:, 0, 1, :], in0=etl[:, 0, 0, :], scalar1=rowmask[:, 1:2], scalar2=None, op0=ALU.mult),
             reads=[t_sc, t_const], writes=[t_sc])
        P.op("dve", lambda e, t=t: e.tensor_scalar(out=etl[:, 0, 0, :], in0=etl[:, 0, 0, :], scalar1=rowmask[:, 0:1], scalar2=None, op0=ALU.mult),
             reads=[t_sc, t_const], writes=[t_sc])
        P.op("act", lambda e, t=t: e.activation(out=egl[:, 0, :, :], in_=c.ps[1][:, 16:32].rearrange("p (a n) -> p a n", a=2), func=AF.Exp),
             reads=[c.t_ps[1]], writes=[t_sc])
        P.op("dve", lambda e, t=t: e.tensor_copy(out=gcT[:, 0, :], in_=c.ps[1][:, 128:256]), reads=[c.t_ps[1]], writes=[t_sc])

    wt = [A("wt%d" % i, [128, 8, 128], BF16) for i in range(2)]
    t_wt = toks(2)
    t_xbuf = toks(16)
    P.op("pool", lambda e: e.memset(xflat[:, :], 0.0), writes=t_xbuf)
    qhT = A("qhT", [128, 4, 512], BF16)
    khT = A("khT", [128, 4, 512], BF16)
    vT = A("vT", [128, 8, 512], BF16)
    zsT = A("zsT", [128, 8, 512], BF16)
    mixblk = A("mixblk", [128, 8, 128], BF16)
    t_qhT, t_khT, t_vT, t_zsT, t_mixblk = toks(5)
    yT = scr[:, 5 * 1024:5 * 1024 + 512]
    sqb = A("sqb", [128, 512], BF16)
    rn = scr[:, 5 * 1024 + 512:6 * 1024]
    t_yT, t_sqb, t_rn = toks(3)
    f32big = lambda n: A(n, [128, 8, 128], F32)
    bfbig = lambda n: A(n, [128, 8, 128], BF16)
    vb, kbg, attn, attnT, Lm, Um, L2, U2, Pm, P2, nwT, qdT = [bfbig("g_%d" % i) for i in range(12)]
    t_vb, t_kbg, t_attn, t_attnT, t_L, t_U, t_L2, t_U2, t_Pm, t_P2, t_nwT, t_qdT = toks(12)
    ktail2 = A("ktail2", [128, 2, 8, 128], BF16)
    t_ktail2 = Tok()
    scr4 = lambda i: scr[:, i * 1024:(i + 1) * 1024].rearrange("p (a n) -> p a n", a=8)
    Dl, tmpf, grow = scr4(0), scr4(1), scr4(2)
    t_Dl, t_tmpf, t_grow = toks(3)
    BDm = scr4(3)
    t_BD = Tok()
    vnew = A("g_vnew", [128, 8, 128], BF16)
    t_vn = toks(2)
    t_zh, t_wh = toks(2), toks(2)
    Sst = A("g_S", [128, 8, 128], F32)
    Sbf = A("g_Sbf", [128, 8, 128], BF16)
    t_S, t_Sbf = toks(2), toks(2)
    P.op("pool", lambda e: e.memset(Sst[:], 0.0), writes=t_S)
    P.op("pool", lambda e: e.memset(Sbf[:], 0.0), writes=t_Sbf)
    P.op("pool", lambda e: e.memset(vnew[:], 0.0), writes=t_vn)
    sq8 = bfbig("g_sq8")
    rs8 = scr4(4)
    t_sq8, t_rs8 = toks(2)
    ps2 = lambda k: c.psall[:, k * 1024:(k + 1) * 1024]
    t_ps2 = [c.t_pspair[k] for k in range(4)]
    X, Y, Z, W = 0, 1, 2, 3
    Xall = [t_ps2[X]]
    Zall = [t_ps2[Z]] + t_zh
    Wall = [t_ps2[W]] + t_wh

    def v3(ap):
        return ap.rearrange("p (a n) -> p a n", a=8)

    wi = 0
    for tb in range(nblocks):
        bcols = slice(tb * 512, (tb + 1) * 512)
        for ct in range(24):
            wb = wi % 2
            wi += 1
            load_w(c, win, ct * 128, 128, wt[wb], t_wt[wb])
            bank = 4 + (ct % 2)
            proj_fm(c, wt[wb], t_wt[wb], hnT, t_hnT, tb, bank, wtoks=Zall)
            if ct >= 16:
                P.op("act", lambda e, ct=ct, bank=bank: e.activation(out=zsT[:, ct - 16, :], in_=c.ps[bank][:, :], func=AF.Silu),
                     reads=[*Zall, *Zall], writes=[t_zsT])
                continue
            P.op("dve", lambda e, ct=ct: e.tensor_copy(out=xbuf[:, ct, 0:3], in_=xbuf[:, ct, 512:515]), reads=[t_xbuf[ct]], writes=[t_xbuf[ct]])
            P.op("act", lambda e, ct=ct, bank=bank: e.activation(out=xbuf[:, ct, 3:515], in_=c.ps[bank][:, :], func=AF.Copy),
                 reads=[*Zall, *Zall], writes=[t_xbuf[ct]])
            cb = 6 + (ct % 2)
            for tap in range(4):
                P.op("pe", lambda e, ct=ct, tap=tap, cb=cb: e.matmul(c.ps[cb][:, :], lhsT=dg[:, ct * 4 + tap, :], rhs=xbuf[:, ct, tap:tap + 512],
                                                                    start=(tap == 0), stop=(tap == 3)),
                     reads=[t_xbuf[ct], t_const], writes=[*Wall])
            if ct >= 8:
                P.op("act", lambda e, ct=ct, cb=cb: e.activation(out=vT[:, ct - 8, :], in_=c.ps[cb][:, :], func=AF.Silu),
                     reads=[*Wall, *Wall], writes=[t_vT])
                continue
            P.op("act", lambda e, cb=cb: e.activation(out=yT[:], in_=c.ps[cb][:, :], func=AF.Silu), reads=[*Wall, *Wall], writes=[t_yT])
            P.op("act", lambda e: e.activation(out=sqb[:], in_=yT[:], func=AF.Square), reads=[t_yT], writes=[t_sqb])
            P.op("pe", lambda e, cb=cb: e.matmul(c.ps[cb][:, :], lhsT=onesb[:, :], rhs=sqb[:], start=True, stop=True),
                 reads=[t_sqb, t_const], writes=[*Wall])
            P.op("act", lambda e, cb=cb: e.activation(out=rn[:], in_=c.ps[cb][:, :], func=AF.Ln, bias=EPS), reads=[*Wall], writes=[t_rn])
            P.op("act", lambda e: e.activation(out=rn[:], in_=rn[:], func=AF.Exp, scale=-0.5), reads=[t_rn], writes=[t_rn])
            if ct < 4:
                P.op("dve", lambda e, ct=ct: e.scalar_tensor_tensor(out=qhT[:, ct, :], in0=yT[:], scalar=float(128 ** -0.5), in1=rn[:],
                                                                   op0=ALU.mult, op1=ALU.mult), reads=[t_yT, t_rn], writes=[t_qhT])
            else:
                P.op("dve", lambda e, ct=ct: e.tensor_tensor(out=khT[:, ct - 4, :], in0=yT[:], in1=rn[:], op=ALU.mult),
                     reads=[t_yT, t_rn], writes=[t_khT])
        for tt in range(4):
            t = tb * 4 + tt
            tk = slice(tt * 128, (tt + 1) * 128)
            scalars(t)
            zb = c.psall.bitcast(BF16)[:, Z * 2048:(Z + 1) * 2048]
            for hk in range(4):
                P.op("pe", lambda e, hk=hk, tk=tk, zb=zb: e.transpose(out=zb[:, hk * 128:(hk + 1) * 128], in_=khT[:, hk, tk], identity=c.ident[:]),
                     reads=[t_khT, c.t_ident], writes=[*Zall, *Zall, *Zall])
            for hv in range(8):
                P.op("pe", lambda e, hv=hv, tk=tk, zb=zb: e.transpose(out=zb[:, 512 + hv * 128:512 + (hv + 1) * 128], in_=vT[:, hv, tk], identity=c.ident[:]),
                     reads=[t_vT, c.t_ident], writes=[*Zall, *Zall, *Zall])
            kt4 = zb[:, 0:512].rearrange("p (a n) -> p a n", a=4).unsqueeze(2).broadcast_to([128, 4, 2, 128])
            P.op("dve", lambda e, zb=zb, t=t: e.tensor_tensor(out=vb[:, :, :], in0=zb[:, 512:1536].rearrange("p (a n) -> p a n", a=8),
                                                            in1=beta[:, 0, :].unsqueeze(2).broadcast_to([128, 8, 128]), op=ALU.mult),
                 reads=[*Zall, t_beta], writes=[t_vb])
            P.op("dve", lambda e, kt4=kt4, t=t: e.tensor_tensor(out=kbg[:, :, :].rearrange("p (a b) n -> p a b n", b=2), in0=kt4,
                                                              in1=bgam[:, 0, :].rearrange("p (a b) -> p a b", b=2).unsqueeze(3).broadcast_to([128, 4, 2, 128]),
                                                              op=ALU.mult), reads=[*Zall, t_sc], writes=[t_kbg])
            for cc in range(2):
                P.op("dve", lambda e, kt4=kt4, t=t, cc=cc: e.tensor_tensor(
                    out=ktail2[:, cc, :, :].rearrange("p (a b) n -> p a b n", b=2), in0=kt4,
                    in1=etl[:, 0, cc, :].rearrange("p (a b) -> p a b", b=2).unsqueeze(3).broadcast_to([128, 4, 2, 128]), op=ALU.mult),
                    reads=[*Zall, t_sc], writes=[t_ktail2])
            for hk in range(4):
                P.op("pe", lambda e, hk=hk, tk=tk: e.matmul(c.ps[0][:, hk * 128:(hk + 1) * 128], lhsT=khT[:, hk, tk], rhs=khT[:, hk, tk],
                                                           start=True, stop=True), reads=[t_khT], writes=[c.t_ps[0], t_ps2[X]])
            for hk in range(4):
                P.op("pe", lambda e, hk=hk, tk=tk: e.matmul(c.ps[1][:, hk * 128:(hk + 1) * 128], lhsT=qhT[:, hk, tk], rhs=khT[:, hk, tk],
                                                           start=True, stop=True), reads=[t_qhT, t_khT], writes=[c.t_ps[1], t_ps2[X]])
            P.op("pool", lambda e, t=t: e.tensor_tensor(out=BDm[:, :, :], in0=gcT[:, 0, :].unsqueeze(1).broadcast_to([128, 8, 128]),
                                                       in1=delta[:, :].unsqueeze(2).broadcast_to([128, 8, 128]), op=ALU.mult),
                 reads=[t_sc, t_const], writes=[t_BD])
            for hf in range(2):
                P.op("pe", lambda e, hf=hf: e.matmul(c.ps[2 + hf][:, :], lhsT=onesf[:, :], rhs=BDm[:, hf * 4:(hf + 1) * 4, :], start=True, stop=True),
                     reads=[t_BD, t_const], writes=[c.t_ps[2 + hf], t_ps2[Y]])
            P.op("dve", lambda e, t=t: e.tensor_tensor(out=tmpf[:, :, :], in0=gc[:, 0, :].unsqueeze(2).broadcast_to([128, 8, 128]),
                                                      in1=v3(ps2(Y)), op=ALU.subtract), reads=[t_ps2[Y], t_sc], writes=[t_tmpf])
            P.op("pool", lambda e: e.tensor_scalar(out=tmpf[:, :, :], in0=tmpf[:, :, :], scalar1=0.0, scalar2=None, op0=ALU.min),
                 reads=[t_tmpf], writes=[t_tmpf])
            P.op("act", lambda e: e.activation(out=Dl[:, :, :], in_=tmpf[:, :, :], func=AF.Exp), reads=[t_tmpf], writes=[t_Dl])
            P.op("act", lambda e: e.activation(out=grow[:, :, :], in_=v3(ps2(Y)), func=AF.Exp), reads=[t_ps2[Y]], writes=[t_grow])
            P.op("dve", lambda e, tk=tk: e.tensor_tensor(out=qdT[:, :, :].rearrange("p (a b) n -> p a b n", b=2),
                                                        in0=qhT[:, :, tk].unsqueeze(2).broadcast_to([128, 4, 2, 128]),
                                                        in1=grow[:, :, :].rearrange("p (a b) n -> p a b n", b=2), op=ALU.mult),
                 reads=[t_qhT, t_grow], writes=[t_qdT])
            kk4 = c.ps[0][:, :].rearrange("p (a n) -> p a n", a=4).unsqueeze(2).broadcast_to([128, 4, 2, 128])
            qk4 = c.ps[1][:, :].rearrange("p (a n) -> p a n", a=4).unsqueeze(2).broadcast_to([128, 4, 2, 128])
            D4 = Dl[:, :, :].rearrange("p (a b) n -> p a b n", b=2)
            P.op("dve", lambda e, kk4=kk4, D4=D4: e.tensor_tensor(out=tmpf[:, :, :].rearrange("p (a b) n -> p a b n", b=2), in0=kk4, in1=D4, op=ALU.mult),
                 reads=[t_ps2[X], t_Dl], writes=[t_tmpf])
            P.op("pool", lambda e, t=t: e.tensor_tensor(out=tmpf[:, :, :], in0=tmpf[:, :, :], in1=beta[:, 0, :].unsqueeze(2).broadcast_to([128, 8, 128]),
                                                       op=ALU.mult), reads=[t_tmpf, t_beta], writes=[t_tmpf])
            P.op("pool", lambda e: e.tensor_tensor(out=Lm[:, :, :], in0=tmpf[:, :, :], in1=mstrict[:, :].unsqueeze(1).broadcast_to([128, 8, 128]),
                                                  op=ALU.mult), reads=[t_tmpf, t_const], writes=[t_L])
            P.op("dve", lambda e, qk4=qk4, D4=D4: e.tensor_tensor(out=grow[:, :, :].rearrange("p (a b) n -> p a b n", b=2), in0=qk4, in1=D4, op=ALU.mult),
                 reads=[t_ps2[X], t_Dl, t_qdT], writes=[t_grow])
            P.op("pool", lambda e: e.tensor_tensor(out=attn[:, :, :], in0=grow[:, :, :], in1=mcausal[:, :].unsqueeze(1).broadcast_to([128, 8, 128]),
                                                  op=ALU.mult), reads=[t_grow, t_const], writes=[t_attn])
            wbv = c.psall.bitcast(BF16)[:, W * 2048:(W + 1) * 2048]
            for hv in range(8):
                P.op("pe", lambda e, hv=hv, zb=zb: e.transpose(out=zb[:, hv * 128:(hv + 1) * 128], in_=Lm[:, hv, :], identity=c.ident[:]),
                     reads=[t_L, c.t_ident], writes=[*Zall, *Zall, *Zall])
            for hv in range(8):
                P.op("pe", lambda e, hv=hv, wbv=wbv: e.transpose(out=wbv[:, hv * 128:(hv + 1) * 128], in_=attn[:, hv, :], identity=c.ident[:]),
                     reads=[t_attn, c.t_ident], writes=[*Wall, *Wall, *Wall])
            P.op("act", lambda e, zb=zb: e.activation(out=Um[:, :, :], in_=zb[:, 0:1024].rearrange("p (a n) -> p a n", a=8), func=AF.Copy),
                 reads=[*Zall], writes=[t_U])
            P.op("act", lambda e, wbv=wbv: e.activation(out=attnT[:, :, :], in_=wbv[:, 0:1024].rearrange("p (a n) -> p a n", a=8), func=AF.Copy),
                 reads=[*Wall], writes=[t_attnT])
            P.op("dve", lambda e: e.tensor_tensor(out=Pm[:, :, :], in0=c.ident[:, :].unsqueeze(1).broadcast_to([128, 8, 128]), in1=Um[:, :, :],
                                                 op=ALU.subtract), reads=[t_U, c.t_ident], writes=[t_Pm])
            Lc, Uc, Ln_, Un_, Pc, Pn = Lm, Um, L2, U2, Pm, P2
            tLc, tUc, tLn, tUn, tPc, tPn = t_L, t_U, t_L2, t_U2, t_Pm, t_P2
            for lev in range(5):
                for hv in range(8):
                    P.op("pe", lambda e, hv=hv, Lc=Lc, Uc=Uc: e.matmul(c.ps[(hv // 4)][:, (hv % 4) * 128:(hv % 4 + 1) * 128], lhsT=Uc[:, hv, :], rhs=Lc[:, hv, :],
                                                                      start=True, stop=True),
                         reads=[tLc, tUc], writes=[t_ps2[X], c.t_ps[hv // 4]])
                if lev < 4:
                    for hv in range(8):
                        P.op("pe", lambda e, hv=hv, Lc=Lc, Uc=Uc: e.matmul(c.ps[2 + (hv // 4)][:, (hv % 4) * 128:(hv % 4 + 1) * 128], lhsT=Lc[:, hv, :],
                                                                          rhs=Uc[:, hv, :], start=True, stop=True),
                             reads=[tLc, tUc], writes=[t_ps2[Y], c.t_ps[2 + hv // 4]])
                P.op("act", lambda e, Ln_=Ln_: e.activation(out=Ln_[:, :, :], in_=v3(ps2(X)), func=AF.Copy), reads=[t_ps2[X]], writes=[tLn])
                if lev < 4:
                    P.op("dve", lambda e, Un_=Un_: e.tensor_copy(out=Un_[:, :, :], in_=v3(ps2(Y))), reads=[t_ps2[Y]], writes=[tUn])
                for hv in range(8):
                    P.op("pe", lambda e, hv=hv, Ln_=Ln_, Pc=Pc: e.matmul(c.ps[4 + (hv // 4)][:, (hv % 4) * 128:(hv % 4 + 1) * 128], lhsT=Ln_[:, hv, :],
                                                                        rhs=Pc[:, hv, :], start=True, stop=True),
                         reads=[tLn, tPc], writes=[*Zall, *Zall])
                P.op("dve", lambda e, Pc=Pc, Pn=Pn: e.tensor_tensor(out=Pn[:, :, :], in0=v3(ps2(Z)), in1=Pc[:, :, :], op=ALU.add),
                     reads=[*Zall, tPc], writes=[tPn])
                Lc, Ln_, tLc, tLn = Ln_, Lc, tLn, tLc
                Uc, Un_, tUc, tUn = Un_, Uc, tUn, tUc
                Pc, Pn, tPc, tPn = Pn, Pc, tPn, tPc
            TT, tTT = Pc, tPc
            for hv in range(8):
                P.op("pe", lambda e, hv=hv, TT=TT: e.matmul(c.ps[2 + (hv // 4)][:, (hv % 4) * 128:(hv % 4 + 1) * 128], lhsT=kbg[:, hv, :], rhs=TT[:, hv, :],
                                                           start=True, stop=True), reads=[t_kbg, tTT], writes=[t_ps2[Y], c.t_ps[2 + hv // 4]])
            P.op("act", lambda e: e.activation(out=nwT[:, :, :], in_=v3(ps2(Y)), func=AF.Copy, scale=-1.0), reads=[t_ps2[Y]], writes=[t_nwT])
            for cc in range(2):
                rr = slice(cc * 64, (cc + 1) * 64)
                for g in range(2):
                    g4 = slice(4 * g, 4 * g + 4)
                    for hv in range(4 * g, 4 * g + 4):
                        zs = c.ps[4 + g][:, (hv % 4) * 128:(hv % 4 + 1) * 128]
                        P.op("pe", lambda e, hv=hv, TT=TT, zs=zs: e.matmul(zs, lhsT=TT[:, hv, :], rhs=vb[:, hv, :], start=True, stop=False),
                             reads=[tTT, t_vb], writes=[t_zh[g]])
                        P.op("pe", lambda e, hv=hv, zs=zs: e.matmul(zs, lhsT=nwT[:, hv, :], rhs=Sbf[:, hv, :], start=False, stop=True),
                             reads=[t_nwT, t_Sbf[g]], writes=[t_zh[g]])
                    zin = c.ps[4 + g][rr, :].rearrange("p (a n) -> p a n", a=4)
                    if g == 0:
                        P.op("act", lambda e, rr=rr, g4=g4, zin=zin: e.activation(out=vnew[rr, g4, :], in_=zin, func=AF.Copy),
                             reads=[t_zh[g]], writes=[t_vn[g]])
                    else:
                        P.op("dve", lambda e, rr=rr, g4=g4, zin=zin: e.tensor_copy(out=vnew[rr, g4, :], in_=zin),
                             reads=[t_zh[g]], writes=[t_vn[g]])
                for g in range(2):
                    g4 = slice(4 * g, 4 * g + 4)
                    for hv in range(4 * g, 4 * g + 4):
                        ob = c.ps[hv // 4]
                        o0 = (hv % 4) * 128
                        oc = slice(o0 + cc * 64, o0 + (cc + 1) * 64)
                        ws = c.ps[6 + g][:, (hv % 4) * 128:(hv % 4 + 1) * 128]
                        P.op("pe", lambda e, hv=hv, ob=ob, oc=oc, cc=cc: e.matmul(ob[:, oc], lhsT=Sbf[:, hv, :], rhs=qdT[:, hv, cc * 64:(cc + 1) * 64],
                                                                                 start=True, stop=False),
                             reads=[t_Sbf[g], t_qdT], writes=[*Xall])
                        P.op("pe", lambda e, hv=hv, ob=ob, oc=oc, cc=cc: e.matmul(ob[:, oc], lhsT=vnew[:, hv, :], rhs=attnT[:, hv, cc * 64:(cc + 1) * 64],
                                                                                 start=False, stop=True),
                             reads=[t_vn[g], t_attnT], writes=[*Xall])
                        P.op("pe", lambda e, hv=hv, cc=cc, ws=ws: e.matmul(ws, lhsT=ktail2[:, cc, hv, :], rhs=vnew[:, hv, :], start=True, stop=True),
                             reads=[t_ktail2, t_vn[g]], writes=[t_wh[g]])
                    win4 = c.ps[6 + g][:, :].rearrange("p (a n) -> p a n", a=4)
                    P.op("dve", lambda e, g4=g4, cc=cc: e.tensor_tensor(out=Sst[:, g4, :], in0=Sst[:, g4, :],
                                                                        in1=egl[:, 0, cc, g4].unsqueeze(2).broadcast_to([128, 4, 128]), op=ALU.mult),
                         reads=[t_S[g], t_sc], writes=[t_S[g]])
                    P.op("dve", lambda e, g4=g4, win4=win4: e.tensor_tensor(out=Sst[:, g4, :], in0=Sst[:, g4, :], in1=win4, op=ALU.add),
                         reads=[t_S[g], t_wh[g]], writes=[t_S[g]])
                    if g == 0:
                        P.op("act", lambda e, g4=g4: e.activation(out=Sbf[:, g4, :], in_=Sst[:, g4, :], func=AF.Copy), reads=[t_S[g]], writes=[t_Sbf[g]])
                    else:
                        P.op("pool", lambda e, g4=g4: e.tensor_copy(out=Sbf[:, g4, :], in_=Sst[:, g4, :]), reads=[t_S[g]], writes=[t_Sbf[g]])
            P.op("act", lambda e: e.activation(out=sq8[:, :, :], in_=v3(ps2(X)), func=AF.Square), reads=[t_ps2[X]], writes=[t_sq8])
            for hf in range(2):
                P.op("pe", lambda e, hf=hf: e.matmul(c.ps[2 + hf][:, :], lhsT=onesb[:, :], rhs=sq8[:, hf * 4:(hf + 1) * 4, :], start=True, stop=True),
                     reads=[t_sq8, t_const], writes=[t_ps2[Y], c.t_ps[2 + hf]])
            P.op("act", lambda e: e.activation(out=rs8[:, :, :], in_=v3(ps2(Y)), func=AF.Ln, scale=1.0 / 128, bias=EPS), reads=[t_ps2[Y]], writes=[t_rs8])
            P.op("act", lambda e: e.activation(out=rs8[:, :, :], in_=rs8[:, :, :], func=AF.Exp, scale=-0.5), reads=[t_rs8], writes=[t_rs8])
            P.op("dve", lambda e: e.tensor_tensor(out=rs8[:, :, :], in0=v3(ps2(X)), in1=rs8[:, :, :], op=ALU.mult), reads=[t_ps2[X], t_rs8], writes=[t_rs8])
            P.op("dve", lambda e, tk=tk: e.scalar_tensor_tensor(out=mixblk[:, :, :], in0=rs8[:, :, :], scalar=onw[:, 0:1], in1=zsT[:, :, tk],
                                                               op0=ALU.mult, op1=ALU.mult), reads=[t_rs8, t_zsT, t_const], writes=[t_mixblk])
            P.dma("sp", lambda e, t=t: e.dma_start(out=mixT[:, :, t * 128:(t + 1) * 128].rearrange("c p n -> p c n"), in_=mixblk[:, :, :]),
                  reads=[t_mixblk])

    if nblocks < 8:
        for t in range(nblocks * 4, NT):
            P.dma("sp", lambda e, t=t: e.dma_start(out=mixT[:, :, t * 128:(t + 1) * 128].rearrange("c p n -> p c n"), in_=mixblk[:, :, :]))
    P.barrier()
    outproj(c, mixT, None, wout, pout, hnT[:, :, :].rearrange("p c n -> p (c n)"), t_pout=(fused or {}).get("t_pout"))
    if fused is None:
        P.emit()
    return nc


def odd_consts():
    bf = ml_dtypes.bfloat16
    a = np.arange(128)[:, None]
    b = np.arange(128)[None, :]
    same = (a // 64) == (b // 64)
    U = (same & (a <= b)).astype(np.float32)
    Aft = (same & (a > b)).astype(np.float32)
    csel = np.zeros((128, 256), np.float32)
    csel[0:64, 0:128] = 1.0
    csel[64:128, 128:256] = 1.0
    mstrict = (same & (a > b)).astype(np.float32)
    mcausal = (same & (a >= b)).astype(np.float32)
    delta = np.zeros((128, 8), np.float32)
    delta[np.arange(8), np.arange(8)] = 1.0
    rowmask = np.zeros((128, 2), np.float32)
    rowmask[0:64, 0] = 1.0
    rowmask[64:128, 1] = 1.0
    return {"tri": np.concatenate([U, Aft, np.zeros((128, 128), np.float32)], axis=1), "csel": csel, "mstrict": mstrict.astype(bf),
            "mcausal": mcausal.astype(bf), "delta": delta, "rowmask": rowmask, "ident": np.eye(128, dtype=np.float32).astype(bf)}


def odd_inputs(hprev, p0, p1, norm_w_l, w_in, conv_w, dt_bias, a_log, onw, w_out, j):
    q = w_in[:, 0 + j * 512: 0 + (j + 1) * 512]
    k = w_in[:, 1024 + j * 512: 1024 + (j + 1) * 512]
    v = w_in[:, 2048 + j * 1024: 2048 + (j + 1) * 1024]
    z = w_in[:, 4096 + j * 1024: 4096 + (j + 1) * 1024]
    be = w_in[:, 6144 + j * 8: 6144 + (j + 1) * 8]
    al = w_in[:, 6160 + j * 8: 6160 + (j + 1) * 8]
    pad = np.zeros((D, 112), np.float32)
    win = np.ascontiguousarray(np.concatenate([q, k, v, z, be, al, pad], axis=1))
    cwc = np.concatenate([conv_w[:, 0 + j * 512: (j + 1) * 512], conv_w[:, 1024 + j * 512: 1024 + (j + 1) * 512],
                          conv_w[:, 2048 + j * 1024: 2048 + (j + 1) * 1024]], axis=1)
    cw = np.ascontiguousarray(cwc.reshape(4, 16, 128).transpose(2, 1, 0).reshape(128, 64))
    hc = np.concatenate([dt_bias[j * 8:(j + 1) * 8], a_log[j * 8:(j + 1) * 8]]).reshape(1, 16).astype(np.float32)
    m = {"hprev": hprev, "p0": p0, "p1": p1, "nw": norm_w_l.reshape(1, D), "win": win,
         "wout": np.ascontiguousarray(w_out[j * 1024:(j + 1) * 1024]), "cw": cw, "hconst": hc,
         "onw": np.ascontiguousarray(onw.reshape(128, 1))}
    m.update(odd_consts())
    return m


def build_final():
    nc = bass.Bass("TRN2", target_bir_lowering=False)
    P = Prog(nc)
    H = S // 2
    dram = lambda n, shp, dt=F32: nc.dram_tensor(n, shp, dt, kind="ExternalInput").ap()
    hprev = dram("hprev", [H, D])
    p0 = dram("p0", [H, D])
    p1 = dram("p1", [H, D])
    nw = dram("nw", [1, D])
    out = nc.dram_tensor("out", [H, D], F32, kind="ExternalOutput").ap()
    c = common_setup(nc, P, None)
    phase1(c, hprev, p0, p1, nw, out, None, None, ntiles=H // 128)
    P.emit()
    return nc


def build_fused(pairs=((0, 1), (2, 3), (4, 5), (6, 7)), use_cc=True, nlayers=4):
    pairs = [list(p) for p in pairs]
    nc = bass.Bass("TRN2", target_bir_lowering=False)
    P = Prog(nc)
    x = nc.dram_tensor("x", [S, D], F32, kind="ExternalInput").ap()
    fnw = nc.dram_tensor("fnw", [1, D], F32, kind="ExternalInput").ap()
    out = nc.dram_tensor("out", [S, D], F32, kind="ExternalOutput").ap()
    h_d = nc.dram_tensor("h_d", [S, D], F32).ap()
    pout_d = nc.dram_tensor("pout_d", [S, D], F32).ap()
    psum_d = nc.dram_tensor("psum_d", [S, D], F32).ap()
    mixT = nc.dram_tensor("mixT", [8, 128, S], BF16).ap()
    c = common_setup(nc, P, None)
    for layer in range(nlayers):
        fz = {"nc": nc, "P": P, "c": c, "pre": "L%d_" % layer, "hprev": (x if layer == 0 else h_d), "psum": (None if layer == 0 else psum_d),
              "hout": h_d, "pout": pout_d, "mixT": mixT, "t_pout": toks(4)}
        if layer % 2 == 0:
            build_even(fused=fz)
        else:
            build_odd(fused=fz)
        if use_cc:
            for k, r0 in enumerate(range(0, S, 1024)):
                P.cc(lambda e, r0=r0: e.collective_compute("AllReduce", ALU.add, replica_groups=pairs,
                                                           ins=[pout_d[r0:r0 + 1024, :].opt()], outs=[psum_d[r0:r0 + 1024, :].opt()]),
                     writes=[fz["t_pout"][k]])
        else:
            for r0 in range(0, S, 256):
                P.dma("sp", lambda e, r0=r0: e.dma_start(out=psum_d[r0:r0 + 256, :], in_=pout_d[r0:r0 + 256, :]))
        P.barrier()
    c.arena.reset()
    c.t_ps = c.t_ps_single
    phase1(c, h_d, psum_d, None, fnw, out, None, None)
    P.emit()
    return nc


_CACHE = {}


def _prog(name):
    if name not in _CACHE:
        _CACHE[name] = {"even": build_even, "odd": build_odd, "final": build_final, "fused": build_fused}[name]()
    return _CACHE[name]


def fused_inputs(b, j, x, norm_w, final_norm_w, even_w_in, even_w_out, hgrn_lb_logits, hgrn_norm_w,
                 odd_w_in, odd_conv_w, odd_dt_bias, odd_a_log, odd_norm_w, odd_w_out):
    m = {"x": x[b], "fnw": final_norm_w.reshape(1, D)}
    for layer in range(4):
        jl = layer // 2
        if layer % 2 == 0:
            lm = even_inputs(None, None, None, norm_w[layer], even_w_in[jl], even_w_out[jl], hgrn_lb_logits, hgrn_norm_w[jl], jl, j)
        else:
            lm = odd_inputs(None, None, None, norm_w[layer], odd_w_in[jl], odd_conv_w[jl], odd_dt_bias[jl], odd_a_log[jl],
                            odd_norm_w[jl], odd_w_out[jl], j)
        for k, v in lm.items():
            if k in ("hprev", "p0", "p1"):
                continue
            if k == "ident":
                m["ident"] = v
            else:
                m["L%d_%s" % (layer, k)] = v
    return m


def kernel(x, norm_w, final_norm_w, even_w_in, even_w_out, hgrn_lb_logits, hgrn_norm_w,
           odd_w_in, odd_conv_w, odd_dt_bias, odd_a_log, odd_norm_w, odd_w_out):
    f = lambda a: np.ascontiguousarray(np.asarray(a, dtype=np.float32))
    args = [f(a) for a in (x, norm_w, final_norm_w, even_w_in, even_w_out, hgrn_lb_logits, hgrn_norm_w,
                           odd_w_in, odd_conv_w, odd_dt_bias, odd_a_log, odd_norm_w, odd_w_out)]
    B = args[0].shape[0]
    cores = [(b, j) for b in range(B) for j in range(2)]
    maps = [fused_inputs(b, j, *args) for (b, j) in cores]
    res = run_bass_kernel_spmd(_prog("fused"), maps, core_ids=list(range(8))).results
    out = np.stack([np.asarray(res[2 * b]["out"]) for b in range(B)])
    return out.astype(np.float32)
```
